# Optimizing a Trainium2 kernel written in Bass

```python
import jax, jax.numpy as jnp
from jax import lax
import numpy as np

D_MODEL = 1024
BATCH = 2
SEQ = 16384
DEPTH = 4

N_A_LAYERS = DEPTH // 2
N_B_LAYERS = DEPTH - N_A_LAYERS

POOL_WINDOWS = (2, 4, 8, 16)
N_POOL_GROUPS = len(POOL_WINDOWS)
POOL_GROUP = D_MODEL // N_POOL_GROUPS

N_HEADS = 16
QK_NOPE_DIM = 128
QK_ROPE_DIM = 64
QK_DIM = QK_NOPE_DIM + QK_ROPE_DIM
V_DIM = 128
Q_LORA_RANK = 256
KV_LORA_RANK = 128
ROPE_THETA = 10000.0
Q_BLOCK = 128

D_FF = 2816
RMS_EPS = 1e-6

kernel_name = "yoco_pool_mla_macaron_trunk"


def rmsnorm(x, g):
    xf = x.astype(jnp.float32)
    y = xf * lax.rsqrt(jnp.mean(xf * xf, axis=-1, keepdims=True) + RMS_EPS)
    return (y * g.astype(jnp.float32)).astype(x.dtype)


def swiglu(u, wg, wu, wd):
    return (jax.nn.silu(u @ wg) * (u @ wu)) @ wd


def rope_tables(seq):
    pos = jnp.arange(seq, dtype=jnp.float32)
    inv_freq = ROPE_THETA ** (-jnp.arange(0, QK_ROPE_DIM, 2, dtype=jnp.float32) / QK_ROPE_DIM)
    ang = pos[:, None] * inv_freq[None, :]
    return jnp.cos(ang), jnp.sin(ang)


def apply_rope(x, cos, sin):
    half = x.shape[-1] // 2
    x1, x2 = x[..., :half], x[..., half:]
    c, s = cos.astype(x.dtype), sin.astype(x.dtype)
    return jnp.concatenate([x1 * c - x2 * s, x1 * s + x2 * c], axis=-1)


def pool_mixer(u, w_groups, scale):
    B, S, D = u.shape
    uf = u.astype(jnp.float32)
    c = jnp.cumsum(uf, axis=1)
    count = jnp.arange(1, S + 1, dtype=jnp.float32)[None, :, None]
    outs = []
    for g, w in enumerate(POOL_WINDOWS):
        sl = slice(g * POOL_GROUP, (g + 1) * POOL_GROUP)
        cg = c[..., sl]
        lag = jnp.pad(cg, ((0, 0), (w, 0), (0, 0)))[:, :S]
        mean = (cg - lag) / jnp.minimum(count, float(w))
        outs.append(mean - uf[..., sl])
    y = jnp.stack(outs, axis=2).astype(u.dtype)
    z = jnp.einsum('bsgc,gcd->bsgd', y, w_groups).reshape(B, S, D)
    return z * scale


def mla_shared_kv(h, kv_in_norm, w_dkv, ckv_norm, w_uk, w_uv, cos, sin):
    u = rmsnorm(h, kv_in_norm)
    kv_a = u @ w_dkv
    c_kv = rmsnorm(kv_a[..., :KV_LORA_RANK], ckv_norm)
    k_rope = apply_rope(kv_a[..., KV_LORA_RANK:], cos[None], sin[None])
    k_nope = jnp.einsum('bsr,rhd->bshd', c_kv, w_uk)
    v = jnp.einsum('bsr,rhd->bshd', c_kv, w_uv)
    return k_nope, k_rope, v


def mla_attention(u, q_lora_norm, w_dq, w_uq, w_o, k_nope, k_rope, v, cos, sin):
    B, S, _ = u.shape
    cq = rmsnorm(u @ w_dq, q_lora_norm)
    q = jnp.einsum('bsr,rhd->bshd', cq, w_uq)
    q_nope = q[..., :QK_NOPE_DIM]
    q_rope = apply_rope(q[..., QK_NOPE_DIM:], cos[None, :, None], sin[None, :, None])
    nblk = S // Q_BLOCK
    qn = q_nope.reshape(B, nblk, Q_BLOCK, N_HEADS, QK_NOPE_DIM).transpose(1, 0, 2, 3, 4)
    qr = q_rope.reshape(B, nblk, Q_BLOCK, N_HEADS, QK_ROPE_DIM).transpose(1, 0, 2, 3, 4)
    scale = QK_DIM ** -0.5
    key_pos = jnp.arange(S)

    def block(args):
        i, qn_b, qr_b = args
        s = (jnp.einsum('bqhd,bkhd->bhqk', qn_b, k_nope)
             + jnp.einsum('bqhr,bkr->bhqk', qr_b, k_rope))
        s = s.astype(jnp.float32) * scale
        q_pos = i * Q_BLOCK + jnp.arange(Q_BLOCK)
        mask = key_pos[None, :] <= q_pos[:, None]
        s = jnp.where(mask[None, None], s, -jnp.inf)
        p = jax.nn.softmax(s, axis=-1).astype(v.dtype)
        return jnp.einsum('bhqk,bkhd->bqhd', p, v)

    o = lax.map(block, (jnp.arange(nblk), qn, qr))
    o = o.transpose(1, 0, 2, 3, 4).reshape(B, S, N_HEADS * V_DIM)
    return o @ w_o


def setup_inputs(seed: int = 0) -> dict:
    key = jax.random.key(seed)
    ks = jax.random.split(key, 32)
    f32 = jnp.float32

    def w(k, shape, fan_in):
        return jax.random.normal(k, shape, f32) * (fan_in ** -0.5)

    def gain(k, shape):
        return 1.0 + 0.02 * jax.random.normal(k, shape, f32)

    D, F, G, Dg = D_MODEL, D_FF, N_POOL_GROUPS, POOL_GROUP
    return {
        "x": jax.random.normal(ks[0], (BATCH, SEQ, D), f32),
        "ffn_pre_norm": gain(ks[1], (DEPTH, D)),
        "ffn_pre_wg": w(ks[2], (DEPTH, D, F), D),
        "ffn_pre_wu": w(ks[3], (DEPTH, D, F), D),
        "ffn_pre_wd": w(ks[4], (DEPTH, F, D), F),
        "mix_norm": gain(ks[5], (DEPTH, D)),
        "ffn_post_norm": gain(ks[6], (DEPTH, D)),
        "ffn_post_wg": w(ks[7], (DEPTH, D, F), D),
        "ffn_post_wu": w(ks[8], (DEPTH, D, F), D),
        "ffn_post_wd": w(ks[9], (DEPTH, F, D), F),
        "pool_w": w(ks[10], (N_A_LAYERS, G, Dg, Dg), Dg),
        "pool_scale": gain(ks[11], (N_A_LAYERS, D)),
        "kv_in_norm": gain(ks[12], (D,)),
        "w_dkv": w(ks[13], (D, KV_LORA_RANK + QK_ROPE_DIM), D),
        "ckv_norm": gain(ks[14], (KV_LORA_RANK,)),
        "w_uk": w(ks[15], (KV_LORA_RANK, N_HEADS, QK_NOPE_DIM), KV_LORA_RANK),
        "w_uv": w(ks[16], (KV_LORA_RANK, N_HEADS, V_DIM), KV_LORA_RANK),
        "q_lora_norm": gain(ks[17], (N_B_LAYERS, Q_LORA_RANK)),
        "w_dq": w(ks[18], (N_B_LAYERS, D, Q_LORA_RANK), D),
        "w_uq": w(ks[19], (N_B_LAYERS, Q_LORA_RANK, N_HEADS, QK_DIM), Q_LORA_RANK),
        "w_o": w(ks[20], (N_B_LAYERS, N_HEADS * V_DIM, D), N_HEADS * V_DIM),
        "final_norm": gain(ks[21], (D,)),
    }


def reference(x, ffn_pre_norm, ffn_pre_wg, ffn_pre_wu, ffn_pre_wd, mix_norm,
              ffn_post_norm, ffn_post_wg, ffn_post_wu, ffn_post_wd,
              pool_w, pool_scale, kv_in_norm, w_dkv, ckv_norm, w_uk, w_uv,
              q_lora_norm, w_dq, w_uq, w_o, final_norm):
    S = x.shape[1]
    cos, sin = rope_tables(S)
    h = x
    k_nope = k_rope = v = None
    for l in range(DEPTH):
        h = h + 0.5 * swiglu(rmsnorm(h, ffn_pre_norm[l]), ffn_pre_wg[l], ffn_pre_wu[l], ffn_pre_wd[l])
        u = rmsnorm(h, mix_norm[l])
        if l < N_A_LAYERS:
            h = h + pool_mixer(u, pool_w[l], pool_scale[l])
        else:
            j = l - N_A_LAYERS
            h = h + mla_attention(u, q_lora_norm[j], w_dq[j], w_uq[j], w_o[j],
                                  k_nope, k_rope, v, cos, sin)
        h = h + 0.5 * swiglu(rmsnorm(h, ffn_post_norm[l]), ffn_post_wg[l], ffn_post_wu[l], ffn_post_wd[l])
        if l == N_A_LAYERS - 1:
            k_nope, k_rope, v = mla_shared_kv(h, kv_in_norm, w_dkv, ckv_norm, w_uk, w_uv, cos, sin)
    return rmsnorm(h, final_norm)
```

```python
import contextlib
import numpy as np
import ml_dtypes
import concourse.bass as bass
import concourse.mybir as mybir
from concourse.bass_utils import run_bass_kernel_spmd

F32 = mybir.dt.float32
BF16 = mybir.dt.bfloat16
AF = mybir.ActivationFunctionType
ALU = mybir.AluOpType

D = 1024
KC = 8
F = 2816
FC = 22
NH = 16
EPS = 1e-6
TOK = 512
HALO = 32
SLAB = 4096
POOL_W = (2, 4, 8, 16)
SM_SCALE = 192.0 ** -0.5


class Op:
    __slots__ = ("eng", "fn", "deps", "signal", "sigval", "chan", "idx", "rawwaits")


class Chan:
    def __init__(self, name):
        self.name = name
        self.sem = None
        self.count = 0
        self.last = None


class Sched:
    ENGS = ("pe", "act", "dve", "pool", "sp")
    BASS = {"pe": "tensor", "act": "scalar", "dve": "vector", "pool": "gpsimd", "sp": "sync"}

    def __init__(self, tag=""):
        self.tag = tag
        self.streams = {e: [] for e in self.ENGS}
        self.last_w = {}
        self.readers = {}
        self.chans = []

    def chan(self, name):
        c = Chan(name)
        self.chans.append(c)
        return c

    def add(self, eng, fn, reads=(), writes=(), chan=None, extra=(), rawwaits=()):
        op = Op()
        op.rawwaits = rawwaits
        op.eng, op.fn, op.chan = eng, fn, chan
        op.signal = chan is not None
        op.sigval = None
        op.idx = len(self.streams[eng])
        deps = {}
        for k in reads:
            w = self.last_w.get(k)
            if w is not None:
                deps[id(w)] = w
        for k in writes:
            w = self.last_w.get(k)
            if w is not None:
                deps[id(w)] = w
            for r in self.readers.get(k, ()):
                deps[id(r)] = r
        for w in extra:
            if w is not None:
                deps[id(w)] = w
        if chan is not None and chan.last is not None:
            deps[id(chan.last)] = chan.last
        dl = []
        for w in deps.values():
            if w is op:
                continue
            if w.chan is None and w.eng == "pe" and eng == "pe" and chan is None:
                continue
            dl.append(w)
        op.deps = dl
        for k in writes:
            self.last_w[k] = op
            self.readers[k] = []
        for k in reads:
            lst = self.readers.setdefault(k, [])
            if chan is None:
                lst[:] = [r for r in lst if not (r.chan is None and r.eng == eng)]
            lst.append(op)
        if chan is not None:
            chan.last = op
            chan.count += 16
            op.sigval = chan.count
        self.streams[eng].append(op)
        return op

    def emit(self, nc, stack, semstack=None):
        semstack = semstack or stack
        for e in self.ENGS:
            for op in self.streams[e]:
                for w in op.deps:
                    w.signal = True
        esem = {}
        for e in self.ENGS:
            esem[e] = semstack.enter_context(nc.semaphore(self.tag + "s_" + e))
            n = 0
            for op in self.streams[e]:
                if op.chan is None and op.signal:
                    n += 1
                    op.sigval = n
        for c in self.chans:
            c.sem = semstack.enter_context(nc.semaphore(self.tag + "c_" + c.name))
        block = stack.enter_context(nc.Block())
        for e in self.ENGS:
            ops = self.streams[e]
            if not ops:
                continue

            def body(eng, ops=ops):
                known = {}
                for op in ops:
                    for (rs, rv) in op.rawwaits:
                        eng.wait_ge(rs, rv)
                    for w in op.deps:
                        sem = w.chan.sem if w.chan is not None else esem[w.eng]
                        key = id(sem)
                        if known.get(key, 0) < w.sigval:
                            eng.wait_ge(sem, w.sigval)
                            known[key] = w.sigval
                    if op.fn is None:
                        continue
                    inst = op.fn(eng)
                    if op.chan is not None:
                        inst.then_inc(op.chan.sem, 16)
                    elif op.signal:
                        inst.then_inc(esem[op.eng], 1)

            getattr(block, self.BASS[e])(body)


def _consts_layout():
    off = {}
    c = 0
    for name, n in (("ffn_pre_norm", 32), ("mix_norm", 32), ("ffn_post_norm", 32),
                    ("pool_scale", 16), ("kv_in_norm", 8), ("final_norm", 8),
                    ("ckv_norm", 1), ("q_lora_norm", 4)):
        off[name] = c
        c += n
    return off, c


COFF, NCONST = _consts_layout()


def _slab_plan():
    plan = {}
    n = 0
    for l in range(4):
        for which in ("pre", "post"):
            plan[("ffn", l, which)] = n
            n += 19
    for j in range(2):
        plan[("attn", j)] = n
        n += 15
    plan["misc"] = n
    n += 1
    return plan, n


PLAN, NSLAB = _slab_plan()
A_SLABS = [s for l in (0, 1) for w in ("pre", "post") for s in range(PLAN[("ffn", l, w)], PLAN[("ffn", l, w)] + 19)] + [PLAN["misc"]]
B_SLABS = [s for l in (2, 3) for w in ("pre", "post") for s in range(PLAN[("ffn", l, w)], PLAN[("ffn", l, w)] + 19)] + \
          [s for j in (0, 1) for s in range(PLAN[("attn", j)], PLAN[("attn", j)] + 15)]


def host_slabs(inp):
    W = np.zeros((NSLAB, 128, SLAB), np.float32)
    for l in range(4):
        for which in ("pre", "post"):
            b = PLAN[("ffn", l, which)]
            if which == "pre":
                wg_, wu_, wd_ = inp["ffn_pre_wg"], inp["ffn_pre_wu"], inp["ffn_pre_wd"]
            else:
                wg_, wu_, wd_ = inp["ffn_post_wg"], inp["ffn_post_wu"], inp["ffn_post_wd"]
            wg, wu, wd = np.asarray(wg_[l]), np.asarray(wu_[l]), np.asarray(wd_[l])
            gu = np.stack([wg, wu], 0).reshape(2, KC, 128, FC, 128)
            gu = gu.transpose(3, 2, 0, 1, 4)
            gu = gu.reshape(11, 2, 128, 2, KC, 128).transpose(0, 2, 1, 3, 4, 5)
            W[b:b + 11] = gu.reshape(11, 128, SLAB)
            w2 = wd.reshape(FC, 128, KC, 128).transpose(2, 1, 0, 3)
            W[b + 11:b + 19, :, :FC * 128] = w2.reshape(8, 128, FC * 128)
    wuk = np.asarray(inp["w_uk"])
    wuv = np.asarray(inp["w_uv"])
    for j in range(2):
        b = PLAN[("attn", j)]
        wdq = np.asarray(inp["w_dq"][j]).reshape(KC, 128, 256).transpose(1, 0, 2)
        W[b, :, :KC * 256] = wdq.reshape(128, KC * 256)
        wuq = np.asarray(inp["w_uq"][j]).reshape(2, 128, NH, 192).transpose(1, 0, 2, 3)
        nope = wuq[..., :128]
        W[b + 1, :, :NH * 128] = nope[:, 0].reshape(128, NH * 128)
        W[b + 2, :, :NH * 128] = nope[:, 1].reshape(128, NH * 128)
        rope = wuq[..., 128:]
        swap = np.concatenate([rope[..., 32:], rope[..., :32]], -1)
        rs = np.concatenate([rope, swap], -1)
        W[b + 3, :, :NH * 128] = rs[:, 0].reshape(128, NH * 128)
        W[b + 4, :, :NH * 128] = rs[:, 1].reshape(128, NH * 128)
        W[b + 5, :, :NH * 128] = wuk.transpose(2, 1, 0).reshape(128, NH * 128)
        W[b + 6, :, :NH * 128] = wuv.reshape(128, NH * 128)
        wo = np.asarray(inp["w_o"][j]).reshape(NH, 128, KC, 128).transpose(2, 1, 0, 3)
        W[b + 7:b + 15, :, :NH * 128] = wo.reshape(8, 128, NH * 128)
    b = PLAN["misc"]
    wdkv = np.asarray(inp["w_dkv"]).reshape(KC, 128, 192).transpose(1, 0, 2)
    lat = wdkv[..., :128]
    rope = wdkv[..., 128:]
    swap = np.concatenate([rope[..., 32:], rope[..., :32]], -1)
    blk = np.concatenate([lat, rope, rope, swap, swap], -1)
    W[b, :, :KC * 384] = blk.reshape(128, KC * 384)
    return W


def host_consts(inp):
    C = np.zeros((128, NCONST), np.float32)

    def put(name, arr):
        a = np.asarray(arr, np.float32).reshape(-1, 128).T
        C[:, COFF[name]:COFF[name] + a.shape[1]] = a
    put("ffn_pre_norm", inp["ffn_pre_norm"])
    put("mix_norm", inp["mix_norm"])
    put("ffn_post_norm", inp["ffn_post_norm"])
    put("pool_scale", inp["pool_scale"])
    put("kv_in_norm", inp["kv_in_norm"])
    put("final_norm", inp["final_norm"])
    put("ckv_norm", inp["ckv_norm"])
    put("q_lora_norm", inp["q_lora_norm"])
    return C


def host_poolw(inp):
    pw = np.asarray(inp["pool_w"], np.float32)
    pw = pw.reshape(2, 4, 2, 128, 256).transpose(3, 0, 1, 2, 4)
    return np.ascontiguousarray(pw.reshape(128, 2 * 4 * 2 * 256))


def rope_tables_host(seq):
    pos = np.arange(seq, dtype=np.float64)
    inv = 10000.0 ** (-np.arange(0, 64, 2, dtype=np.float64) / 64.0)
    ang = pos[None, :] * inv[:, None]
    cos = np.cos(ang)
    sin = np.sin(ang)
    c64 = np.concatenate([cos, cos], 0)
    s64 = np.concatenate([-sin, sin], 0)
    cc = np.concatenate([c64, c64], 0).astype(np.float32)
    ss = np.concatenate([s64, s64], 0).astype(np.float32)
    return cc, ss


class Builder:
    def __init__(self, nc, stack, nch, tag="", semstack=None):
        self.nc = nc
        self.stack = stack
        self.semstack = semstack
        self.tag = tag
        self.S = Sched(tag)
        self.nch = nch
        self.psum_rr = 0
        self.ring_cnt = 0
        self.sg_cnt = 0
        self.cp_cnt = 0
        self.bg_on = False
        self.pass_idx = 0
        self.deferred = []

    def sb(self, name, shape, dt):
        return self.stack.enter_context(self.nc.sbuf_tensor(self.tag + "sb_" + name, shape, dt))

    def setup_common(self, nring, nsg=3):
        nc = self.nc
        self.banks = [self.stack.enter_context(nc.psum_tensor(self.tag + "ps%d" % i, [128, 512], F32)) for i in range(8)]
        self.ring = [self.sb("slab%d" % i, [128, SLAB], BF16) for i in range(nring)]
        self.ring_ch = [self.S.chan("ring%d" % i) for i in range(nring)]
        self.sg = [self.sb("sg%d" % i, [128, TOK], F32) for i in range(nsg)]
        self.ones = self.sb("ones", [128, 128], BF16)
        self.consts = self.sb("consts", [128, NCONST], F32)
        self.hT = self.sb("hT", [128, KC, TOK], F32)
        self.uT = self.sb("uT", [128, KC, TOK], BF16)
        self.scr = self.sb("scr", [128, 12288], BF16)
        self.AT = self.scr[:, 0:FC * TOK].rearrange("p (c t) -> p c t", c=FC)
        self.misc_ch = [self.S.chan("misc%d" % i) for i in range(6)]
        self.tile_ch = [self.S.chan("tile%d" % i) for i in range(KC)]
        self.misc_i = 0
        S = self.S
        S.add("dve", lambda e: e.memset(self.ones[:], 1.0), writes=["ones"])
        self.dma("sp", self.consts[:], self.io["consts"], reads=[], writes=["consts"])

    def mch(self):
        self.misc_i += 1
        return self.misc_ch[self.misc_i % len(self.misc_ch)]

    def dma(self, q, out, in_, reads, writes, chan=None):
        chan = chan or self.mch()
        return self.S.add(q, lambda e: e.dma_start(out=out, in_=in_), reads=reads, writes=writes, chan=chan)

    def next_bank(self, allowed=range(8)):
        allowed = list(allowed)
        self.psum_rr += 1
        return allowed[self.psum_rr % len(allowed)]

    def mm(self, bank, out, lhsT, rhs, start, stop, reads):
        self.S.add("pe", lambda e: e.matmul(out, lhsT, rhs, start=start, stop=stop),
                   reads=reads, writes=[("ps", bank)])

    def convert(self, slabs, stages, stage_keys, outs, out_keys):
        S = self.S
        wts, wbf = self.io["wts"], self.io["wbf"]
        ns, no = len(stages), len(outs)
        ld_ch = [S.chan("cvl%d" % i) for i in range(ns)]
        st_ch = [S.chan("cvs%d" % i) for i in range(no)]
        engs = ["pool", "act", "dve"]
        n = len(slabs)
        depth = ns - 1
        for i in range(n + depth):
            if i < n:
                s = slabs[i]
                self.dma("sp", stages[i % ns], wts[self.slab_map[s]], reads=[], writes=stage_keys[i % ns],
                         chan=ld_ch[i % ns])
            j = i - depth
            if j >= 0:
                s = slabs[j]
                st, sk = stages[j % ns], stage_keys[j % ns]
                ob, ok = outs[j % no], out_keys[j % no]
                eng = engs[j % 3]
                if eng == "act":
                    S.add("act", lambda e, ob=ob, st=st: e.activation(ob, st, AF.Copy), reads=sk, writes=ok)
                else:
                    S.add(eng, lambda e, ob=ob, st=st: e.tensor_copy(ob, st), reads=sk, writes=ok)
                self.dma("sp", wbf[s], ob, reads=ok, writes=[("wbf", s)], chan=st_ch[j % no])

    def bg_setup(self, slabs):
        pieces = []
        for s in slabs:
            used = SLAB
            for (key, base) in PLAN.items():
                if key == "misc":
                    continue
                if key[0] == "ffn" and base + 11 <= s < base + 19:
                    used = FC * 128
                if key[0] == "attn" and base <= s < base + 15:
                    used = 2048
            o = 0
            while o < used:
                ln = min(2048, used - o)
                pieces.append((s, o, ln))
                o += ln
        self.bg_pieces = pieces
        self.bg_i = 0
        self.bg_stage = [self.sb("bgs%d" % i, [128, 2048], F32) for i in range(2)]
        self.bg_out = [self.sb("bgo%d" % i, [128, 2048], BF16) for i in range(2)]
        self.bg_lch = [self.S.chan("bgl%d" % i) for i in range(2)]
        self.bg_sch = [self.S.chan("bgs%d" % i) for i in range(2)]

    def bg_tick(self):
        pcs = getattr(self, "bg_pieces", None)
        if not pcs:
            return
        i = self.bg_i
        if i > len(pcs):
            return
        self.bg_i += 1
        S = self.S
        wts, wbf = self.io["wts"], self.io["wbf"]
        if i < len(pcs):
            s, o, ln = pcs[i]
            self.dma("act", self.bg_stage[i % 2][:, 0:ln], wts[self.slab_map[s]][:, o:o + ln], reads=[],
                     writes=[("bgs", i % 2)], chan=self.bg_lch[i % 2])
        j = i - 1
        if j >= 0:
            s, o, ln = pcs[j]
            st, ob = self.bg_stage[j % 2], self.bg_out[j % 2]
            S.add("act", lambda e: e.activation(ob[:, 0:ln], st[:, 0:ln], AF.Copy),
                  reads=[("bgs", j % 2)], writes=[("bgo", j % 2)])
            self.dma("act", wbf[s][:, o:o + ln], ob[:, 0:ln], reads=[("bgo", j % 2)], writes=[("wbf", s)],
                     chan=self.bg_sch[j % 2])

    def fetch(self, s, nelem=SLAB):
        if self.bg_on:
            self.bg_tick()
        i = self.ring_cnt % len(self.ring)
        self.ring_cnt += 1
        buf = self.ring[i]
        self.dma("sp", buf[:, 0:nelem], self.io["wbf"][s][:, 0:nelem], reads=[("wbf", s)],
                 writes=[("slab", i)], chan=self.ring_ch[i])
        return buf, ("slab", i)

    def norm(self, h_aps, h_keys, gains, out_aps, out_keys, N, dim, sq_src=None, sq_keys=None):
        S = self.S
        n = len(h_aps)
        sq_src = sq_src or h_aps
        sq_keys = sq_keys or h_keys
        hsq = self.scr[:, 0:8 * TOK].rearrange("p (c t) -> p c t", c=8)
        for i in range(n):
            S.add("act", lambda e, i=i: e.activation(hsq[:, i, 0:N], sq_src[i], AF.Square),
                  reads=[sq_keys[i]], writes=[("scr", i)])
        b = self.next_bank()
        bk = self.banks[b]
        for i in range(n):
            self.mm(b, bk[:, 0:N], self.ones[:], hsq[:, i, 0:N], i == 0, i == n - 1, reads=["ones", ("scr", i)])
        S.add("act", lambda e: e.activation(bk[:, 0:N], bk[:, 0:N], AF.Sqrt, bias=self.eps[:], scale=1.0 / dim),
              reads=[("ps", b), "eps"], writes=[("ps", b)])
        S.add("dve", lambda e: e.reciprocal(bk[:, 0:N], bk[:, 0:N]), reads=[("ps", b)], writes=[("ps", b)])
        for i in range(n):
            S.add("dve", lambda e, i=i: e.scalar_tensor_tensor(out_aps[i], h_aps[i], gains[i], bk[:, 0:N],
                                                               ALU.mult, ALU.mult),
                  reads=[h_keys[i], ("ps", b), "consts"], writes=[out_keys[i]])

    def cgain(self, name, idx):
        c = COFF[name] + idx
        return self.consts[:, c:c + 1]

    def norm_h(self, gname, gbase, N, out_fp32=None):
        hT, uT = self.hT, self.uT
        h_aps = [hT[:, kc, 0:N] for kc in range(KC)]
        h_keys = [("h", kc) for kc in range(KC)]
        gains = [self.cgain(gname, gbase + kc) for kc in range(KC)]
        if out_fp32 is None:
            outs = [uT[:, kc, 0:N] for kc in range(KC)]
            okeys = [("u", kc) for kc in range(KC)]
        else:
            outs, okeys = out_fp32
        self.norm(h_aps, h_keys, gains, outs, okeys, N, float(D))

    def ffn(self, l, which, N):
        S = self.S
        hT, uT, AT = self.hT, self.uT, self.AT
        self.norm_h("ffn_%s_norm" % which, l * 8, N)
        base = PLAN[("ffn", l, which)]
        for s in range(11):
            buf, skey = self.fetch(base + s)
            v = buf[:].rearrange("p (a g k f) -> p a g k f", a=2, g=2, k=KC)
            for fcl in range(2):
                fc = 2 * s + fcl
                bg = self.next_bank()
                for kc in range(KC):
                    self.mm(bg, self.banks[bg][:, 0:N], v[:, fcl, 0, kc, :], uT[:, kc, 0:N], kc == 0, kc == KC - 1,
                            reads=[skey, ("u", kc)])
                bu = self.next_bank()
                for kc in range(KC):
                    self.mm(bu, self.banks[bu][:, 0:N], v[:, fcl, 1, kc, :], uT[:, kc, 0:N], kc == 0, kc == KC - 1,
                            reads=[skey, ("u", kc)])
                self.sg_cnt += 1
                si = self.sg_cnt % len(self.sg)
                sg = self.sg[si]
                S.add("act", lambda e, bg=bg, sg=sg: e.activation(sg[:, 0:N], self.banks[bg][:, 0:N], AF.Silu),
                      reads=[("ps", bg)], writes=[("sg", si)])
                S.add("dve", lambda e, bu=bu, sg=sg, fc=fc: e.tensor_tensor(AT[:, fc, 0:N], self.banks[bu][:, 0:N],
                                                                            sg[:, 0:N], ALU.mult),
                      reads=[("ps", bu), ("sg", si)], writes=[("scr", fc)])
        for dc in range(KC):
            buf, skey = self.fetch(base + 11 + dc, FC * 128)
            v = buf[:, 0:FC * 128].rearrange("p (c d) -> p c d", c=FC)
            bo = self.next_bank()
            for fc in range(FC):
                self.mm(bo, self.banks[bo][:, 0:N], v[:, fc, :], AT[:, fc, 0:N], fc == 0, fc == FC - 1,
                        reads=[skey, ("scr", fc)])
            S.add("dve", lambda e, bo=bo, dc=dc: e.scalar_tensor_tensor(hT[:, dc, 0:N], self.banks[bo][:, 0:N], 0.5,
                                                                        hT[:, dc, 0:N], ALU.mult, ALU.add),
                  reads=[("ps", bo), ("h", dc)], writes=[("h", dc)])

    def phase_a_setup(self):
        nch = self.nch
        self.PADW = 16
        self.ub = self.sb("ub", [128, KC, 16 + TOK], F32)
        self.tmp = [self.sb("ptmp%d" % i, [128, 2, 16 + TOK], F32) for i in range(4)]
        self.yT = self.sb("yT", [128, KC, TOK], BF16)
        self.halo_u = self.sb("halo_u", [128, 2, KC, nch * HALO], F32)
        self.poolw = self.sb("poolw", [128, 2 * 4 * 2 * 256], BF16)
        self.pcorr = self.sb("pcorr", [128, 4, 16], F32)
        self.wdkv = self.sb("wdkv", [128, KC * 384], BF16)
        self.cc = self.sb("cc", [128, TOK], F32)
        self.ss = self.sb("ss", [128, TOK], F32)
        self.lat = self.sb("lat", [128, TOK], F32)
        self.t1 = self.sb("t1", [128, TOK], F32)
        self.t2 = self.sb("t2", [128, TOK], F32)
        self.kvl_sb = self.sb("kvl_sb", [128, TOK], BF16)
        self.kvr_sb = self.sb("kvr_sb", [128, TOK], BF16)
        self.kvv_sb = self.sb("kvv_sb", [128, TOK], BF16)
        self.ident = self.sb("ident", [128, 128], BF16)
        self.eps = self.sb("eps", [128, 1], F32)
        S = self.S
        S.add("dve", lambda e: e.memset(self.eps[:], EPS), writes=["eps"])
        ubflat = self.ub[:].rearrange("p a b -> p (a b)")[:, 0:SLAB]
        ubkeys = [("ub", kc) for kc in range(KC)] + [("ubpad", 0)]
        self.dma("sp", ubflat, self.io["poolw"], reads=[], writes=ubkeys)
        S.add("dve", lambda e: e.tensor_copy(self.poolw[:], ubflat), reads=ubkeys, writes=["poolw"])
        self.dma("sp", self.pcorr[:], self.io["pcorr"], reads=[], writes=["pcorr"])
        stg2 = self.sb("stgI", [128, 128], F32)
        self.dma("sp", stg2[:], self.io["ident"], reads=[], writes=["stgI"])
        S.add("dve", lambda e: e.tensor_copy(self.ident[:], stg2[:]), reads=["stgI"], writes=["ident"])

    def pool_mix(self, l, N, n, is_halo):
        S = self.S
        ub, yT, hT = self.ub, self.yT, self.hT
        W = 16 + N
        outs = [ub[:, kc, 16:16 + N] for kc in range(KC)]
        okeys = [("ub", kc) for kc in range(KC)]
        self.norm_h("mix_norm", l * 8, N, out_fp32=(outs, okeys))
        if is_halo:
            for kc in range(KC):
                S.add("pool", lambda e, kc=kc: e.tensor_copy(self.halo_u[:, l, kc, 0:N], ub[:, kc, 16:16 + N]),
                      reads=[("ub", kc)], writes=[("halo_u", l)])
            S.add("pool", lambda e: e.memset(ub[:, :, 0:16], 0.0), writes=[("ubpad", 0)])
            if l == 1:
                return
        else:
            S.add("pool", lambda e: e.tensor_copy(ub[:, :, 0:16], self.halo_u[:, l, :, n * HALO + 16:n * HALO + 32]),
                  reads=[("halo_u", l)], writes=[("ubpad", 0)])
        for g in range(4):
            eng = "dve" if g in (0, 3) else "pool"
            ta, tb = (self.tmp[0], self.tmp[1]) if eng == "dve" else (self.tmp[2], self.tmp[3])
            ka, kb = ("tmp", 0 if eng == "dve" else 2), ("tmp", 1 if eng == "dve" else 3)
            src, skeys = ub[:, 2 * g:2 * g + 2, :], [("ub", 2 * g), ("ub", 2 * g + 1), ("ubpad", 0)]
            lo = 0
            for lev in range(g + 1):
                sh = 1 << lev
                nlo = lo + sh
                dst, dkey = (ta, ka) if lev % 2 == 0 else (tb, kb)
                S.add(eng, lambda e, dst=dst, src=src, nlo=nlo, sh=sh: e.tensor_tensor(
                    dst[:, :, nlo:W], src[:, :, nlo:W], src[:, :, nlo - sh:W - sh], ALU.add),
                    reads=skeys, writes=[dkey])
                src, skeys, lo = dst, [dkey], nlo
            if n == 0 and not is_halo:
                for j in range(2):
                    S.add("dve", lambda e, src=src, j=j, g=g: e.tensor_tensor(
                        src[:, j, 16:32], src[:, j, 16:32], self.pcorr[:, g, :], ALU.mult),
                        reads=skeys + ["pcorr"], writes=skeys)
            for j in range(2):
                kc = 2 * g + j
                S.add("dve", lambda e, src=src, j=j, kc=kc, g=g: e.scalar_tensor_tensor(
                    yT[:, kc, 0:N], src[:, j, 16:W], 1.0 / POOL_W[g], ub[:, kc, 16:W], ALU.mult, ALU.subtract),
                    reads=skeys + [("ub", kc)], writes=[("y", kc)])
        pw = self.poolw[:].rearrange("p (l g i d) -> p l g i d", l=2, g=4, i=2)
        for g in range(4):
            for oc in range(2):
                b = self.next_bank()
                for ic in range(2):
                    self.mm(b, self.banks[b][:, 0:N], pw[:, l, g, ic, oc * 128:(oc + 1) * 128], yT[:, 2 * g + ic, 0:N],
                            ic == 0, ic == 1, reads=["poolw", ("y", 2 * g + ic)])
                kc = 2 * g + oc
                sc = self.cgain("pool_scale", l * 8 + kc)
                S.add("dve", lambda e, b=b, kc=kc, sc=sc: e.scalar_tensor_tensor(
                    hT[:, kc, 0:N], self.banks[b][:, 0:N], sc, hT[:, kc, 0:N], ALU.mult, ALU.add),
                    reads=[("ps", b), ("h", kc), "consts"], writes=[("h", kc)])

    def kv_latent(self, n):
        S = self.S
        N = TOK
        uT = self.uT
        io = self.io
        self.norm_h("kv_in_norm", 0, N)
        wd = self.wdkv[:].rearrange("p (k c) -> p k c", k=KC)
        bl, br, bs = self.next_bank(), self.next_bank(), self.next_bank()
        for (b, c0) in ((bl, 0), (br, 128), (bs, 256)):
            for kc in range(KC):
                self.mm(b, self.banks[b][:, 0:N], wd[:, kc, c0:c0 + 128], uT[:, kc, 0:N], kc == 0, kc == KC - 1,
                        reads=["wdkv", ("u", kc)])
        S.add("act", lambda e: e.activation(self.lat[:], self.banks[bl][:, 0:N], AF.Copy),
              reads=[("ps", bl)], writes=["lat"])
        self.norm([self.lat[:]], ["lat"], [self.cgain("ckv_norm", 0)], [self.kvl_sb[:]], ["kvl_sb"], N, 128.0)
        self.dma("sp", self.cc[:], io["ropeK"][0, :, n * TOK:(n + 1) * TOK], reads=[], writes=["cc"])
        self.dma("sp", self.ss[:], io["ropeK"][1, :, n * TOK:(n + 1) * TOK], reads=[], writes=["ss"])
        S.add("dve", lambda e: e.tensor_tensor(self.t1[:], self.banks[br][:, 0:N], self.cc[:], ALU.mult),
              reads=[("ps", br), "cc"], writes=["t1"])
        S.add("dve", lambda e: e.tensor_tensor(self.t2[:], self.banks[bs][:, 0:N], self.ss[:], ALU.mult),
              reads=[("ps", bs), "ss"], writes=["t2"])
        S.add("pool", lambda e: e.tensor_tensor(self.kvr_sb[:], self.t1[:], self.t2[:], ALU.add),
              reads=["t1", "t2"], writes=["kvr_sb"])
        bt = self.next_bank()
        btv = self.banks[bt][:].bitcast(BF16)
        for i in range(4):
            S.add("pe", lambda e, i=i: e.transpose(btv[:, i * 128:(i + 1) * 128], self.kvl_sb[:, i * 128:(i + 1) * 128],
                                                   self.ident[:]),
                  reads=["kvl_sb", "ident"], writes=[("ps", bt)])
        S.add("act", lambda e: e.activation(self.kvv_sb[:], btv[:, 0:512], AF.Copy), reads=[("ps", bt)], writes=["kvv_sb"])
        self.dma("sp", io["kvl"][:, n * TOK:(n + 1) * TOK], self.kvl_sb[:], reads=["kvl_sb"], writes=[("kvl", n)])
        self.dma("sp", io["kvr"][:, n * TOK:(n + 1) * TOK], self.kvr_sb[:], reads=["kvr_sb"], writes=[("kvr", n)])
        self.dma("sp", io["kvv"][n], self.kvv_sb[:], reads=["kvv_sb"], writes=[("kvv", n)])

    def phase_a(self, convert_slabs):
        S = self.S
        nch = self.nch
        io = self.io
        self.setup_common(nring=4)
        hT = self.hT
        self.phase_a_setup()
        hkeys = [("h", kc) for kc in range(KC)]
        ubflat = self.ub[:].rearrange("p a b -> p (a b)")[:, 0:SLAB]
        ubkeys = [("ub", kc) for kc in range(KC)] + [("ubpad", 0)]
        stages = [ubflat, hT[:].rearrange("p a b -> p (a b)"), self.scr[:, 0:8192].bitcast(F32)]
        skeys = [ubkeys, hkeys, [("scr", i) for i in range(16)]]
        now = [s_ for s_ in convert_slabs if s_ in A_SLABS]
        later = [s_ for s_ in convert_slabs if s_ not in A_SLABS]
        self.convert(now, stages, skeys, [r[:] for r in self.ring], [[("slab", i)] for i in range(len(self.ring))])
        if later:
            self.bg_setup(later)
        b = PLAN["misc"]
        self.dma("sp", self.wdkv[:], io["wbf"][b][:, 0:KC * 384], reads=[("wbf", b)], writes=["wdkv"])
        NHc = nch * HALO
        hkeys = [("h", kc) for kc in range(KC)]
        self.dma("sp", hT[:, :, 0:NHc], io["xh"], reads=[], writes=hkeys)
        self.ffn(0, "pre", NHc)
        self.pool_mix(0, NHc, 0, True)
        self.ffn(0, "post", NHc)
        self.ffn(1, "pre", NHc)
        self.pool_mix(1, NHc, 0, True)
        self.bg_on = True
        for n in range(nch):
            for kc in range(KC):
                self.dma("act" if kc % 2 else "sp", hT[:, kc, :], io["xm"][n][:, kc, :], reads=[], writes=[("h", kc)],
                         chan=self.tile_ch[kc])
            for l in (0, 1):
                self.ffn(l, "pre", TOK)
                self.pool_mix(l, TOK, n, False)
                self.ffn(l, "post", TOK)
            self.kv_latent(n)
            for kc in range(KC):
                self.dma("act" if kc % 2 else "sp", io["hA"][n][:, kc, :], hT[:, kc, :], reads=[("h", kc)],
                         writes=[("hA", n, kc)], chan=self.tile_ch[kc])
        while getattr(self, "bg_pieces", None) and self.bg_i <= len(self.bg_pieces):
            self.bg_tick()
        self.bg_on = False

    def phase_b_setup(self):
        nch = self.nch
        NKT = 16 * nch
        self.NKT = NKT
        self.KTl = self.sb("KTl", [128, NKT * 128], BF16)
        self.KRd = self.sb("KRd", [128, NKT * 128], BF16)
        self.V = self.sb("V", [128, NKT * 128], BF16)
        self.cqn = self.sb("cqn", [128, 2, TOK], BF16)
        self.P = [self.sb("P%d" % i, [128, 512], BF16) for i in range(5)]
        self.PS = [self.sb("PS%d" % i, [128, 512], BF16) for i in range(4)]
        self.PQ = [self.sb("PQ%d" % i, [128, 512], BF16) for i in range(4)]
        self.maskE = self.sb("maskE", [128, 19 * 128], BF16)
        self.rl = self.sb("rl", [128, 1024], F32)
        self.cq = self.rl[:].rearrange("p (c t) -> p c t", c=2)
        self.csQ = self.sb("csQ", [128, TOK], F32)
        self.eps = self.sb("eps", [128, 1], F32)
        S = self.S
        io = self.io
        S.add("dve", lambda e: e.memset(self.eps[:], EPS), writes=["eps"])
        self.dma("sp", self.maskE[:], io["maskE"], reads=[], writes=["maskE"])
        self.OT = self.scr[:, 0:8192].rearrange("p (h t) -> p h t", h=NH)
        self.qx = self.sb("qx", [128, 4096], BF16)
        self.QAs = [self.scr[:, 8192:10240], self.qx[:, 0:2048]]
        self.QRs = [self.scr[:, 10240:12288], self.qx[:, 2048:4096]]
        for ci in range(4 * nch):
            r, n = ci % 4, ci // 4
            self.dma("act", self.KTl[:, ci * 512:(ci + 1) * 512], io["kvl_g"][r, :, n * TOK:(n + 1) * TOK],
                     reads=[], writes=[("KTl", ci)])
            self.dma("act", self.KRd[:, ci * 512:(ci + 1) * 512], io["kvr_g"][r, :, n * TOK:(n + 1) * TOK],
                     reads=[], writes=[("KRd", ci)])
            self.dma("act", self.V[:, ci * 512:(ci + 1) * 512], io["kvv_g"][r, n], reads=[], writes=[("V", ci)])

    def evac(self, out, in_, reads, writes):
        self.cp_cnt += 1
        if self.cp_cnt % 2:
            self.S.add("act", lambda e: e.activation(out, in_, AF.Copy), reads=reads, writes=writes)
        else:
            self.S.add("dve", lambda e: e.tensor_copy(out, in_), reads=reads, writes=writes)

    def attention(self, j, n):
        S = self.S
        hT, uT, banks = self.hT, self.uT, self.banks
        base = PLAN[("attn", j)]
        N = TOK
        self.norm_h("mix_norm", (2 + j) * 8, N)
        buf, skey = self.fetch(base, KC * 256)
        v = buf[:, 0:KC * 256].rearrange("p (k c) -> p k c", k=KC)
        cqb = []
        for cc in range(2):
            b = self.next_bank()
            cqb.append(b)
            for kc in range(KC):
                self.mm(b, banks[b][:, 0:N], v[:, kc, cc * 128:(cc + 1) * 128], uT[:, kc, :], kc == 0, kc == KC - 1,
                        reads=[skey, ("u", kc)])
            S.add("act", lambda e, b=b, cc=cc: e.activation(self.cq[:, cc, :], banks[b][:, 0:N], AF.Copy),
                  reads=[("ps", b)], writes=[("rl", cc)])
        self.norm([self.cq[:, cc, :] for cc in range(2)], [("rl", cc) for cc in range(2)],
                  [self.cgain("q_lora_norm", j * 2 + cc) for cc in range(2)],
                  [self.cqn[:, cc, :] for cc in range(2)], [("cqn", cc) for cc in range(2)], N, 256.0)
        OT = self.OT
        J = 16 * (n + 1)
        def proj_steps(qb, pbanks):
            par = qb % 2
            QAx, QRx = self.QAs[par], self.QRs[par]
            QA3x = QAx.rearrange("p (h q) -> p h q", h=NH)
            QR3x = QRx.rearrange("p (h q) -> p h q", h=NH)
            ka = (lambda g: ("scr", 16 + g)) if par == 0 else (lambda g: ("qx", g))
            kr = (lambda g: ("scr", 20 + g)) if par == 0 else (lambda g: ("qx", 4 + g))
            qc = slice(qb * 128, (qb + 1) * 128)
            b0, k0 = self.fetch(base + 1, NH * 128)
            b1, k1 = self.fetch(base + 2, NH * 128)
            w0 = b0[:, 0:NH * 128].rearrange("p (h d) -> p h d", h=NH)
            w1 = b1[:, 0:NH * 128].rearrange("p (h d) -> p h d", h=NH)
            for g in range(4):
                b = self.next_bank(pbanks)
                for hh in range(4):
                    h = 4 * g + hh
                    o = banks[b][:, hh * 128:(hh + 1) * 128]
                    self.mm(b, o, w0[:, h, :], self.cqn[:, 0, qc], True, False, reads=[k0, ("cqn", 0)])
                    self.mm(b, o, w1[:, h, :], self.cqn[:, 1, qc], False, True, reads=[k1, ("cqn", 1)])
                self.evac(QAx[:, g * 512:(g + 1) * 512], banks[b][:, 0:512], [("ps", b)], [ka(g)])
                yield
            bk_, kk = self.fetch(base + 5, NH * 128)
            wk = bk_[:, 0:NH * 128].rearrange("p (h r) -> p h r", h=NH)
            for g in range(4):
                b = self.next_bank(pbanks)
                for hh in range(4):
                    h = 4 * g + hh
                    self.mm(b, banks[b][:, hh * 128:(hh + 1) * 128], wk[:, h, :], QA3x[:, h, :], True, True,
                            reads=[kk, ka(g)])
                self.evac(QAx[:, g * 512:(g + 1) * 512], banks[b][:, 0:512], [("ps", b)], [ka(g)])
                yield
            b0, k0 = self.fetch(base + 3, NH * 128)
            b1, k1 = self.fetch(base + 4, NH * 128)
            w0 = b0[:, 0:NH * 128].rearrange("p (h d) -> p h d", h=NH)
            w1 = b1[:, 0:NH * 128].rearrange("p (h d) -> p h d", h=NH)
            for g in range(4):
                b = self.next_bank(pbanks)
                for hh in range(4):
                    h = 4 * g + hh
                    o = banks[b][:, hh * 128:(hh + 1) * 128]
                    self.mm(b, o, w0[:, h, :], self.cqn[:, 0, qc], True, False, reads=[k0, ("cqn", 0)])
                    self.mm(b, o, w1[:, h, :], self.cqn[:, 1, qc], False, True, reads=[k1, ("cqn", 1)])
                S.add("dve", lambda e, b=b, g=g: e.tensor_tensor(
                    QR3x[:, 4 * g:4 * g + 4, :], banks[b][:, 0:512].rearrange("p (h q) -> p h q", h=4),
                    self.csQ[:, qc].unsqueeze(1).broadcast_to([128, 4, 128]), ALU.mult),
                    reads=[("ps", b), "csQ"], writes=[kr(g)])
                yield

        gen = proj_steps(0, range(8))
        for _ in gen:
            pass
        for qb in range(4):
            qc = slice(qb * 128, (qb + 1) * 128)
            par = qb % 2
            QA, QR = self.QAs[par], self.QRs[par]
            kA = (lambda g: ("scr", 16 + g)) if par == 0 else (lambda g: ("qx", g))
            kR = (lambda g: ("scr", 20 + g)) if par == 0 else (lambda g: ("qx", 4 + g))
            gen = proj_steps(qb + 1, [7]) if qb < 3 else iter(())
            tick_every = max(1, (4 * (16 * n + 13 + qb)) // 14)
            tick_cnt = [0]
            Jq = 16 * n + 13 + qb
            for ps_ in range(2):
                its = [(jt, g) for jt in range(Jq) for g in range(2)]
                nit = len(its)
                NP = len(self.P)

                nq4 = Jq // 4

                def s_part(it):
                    jt, g = its[it]
                    hg = 2 * ps_ + g
                    sbk = 4 + it % 3
                    kcols = slice(jt * 128, (jt + 1) * 128)
                    self.mm(sbk, banks[sbk][:, 0:512], self.KTl[:, kcols], QA[:, hg * 512:(hg + 1) * 512], True, False,
                            reads=[("KTl", jt // 4), kA(hg)])
                    self.mm(sbk, banks[sbk][:, 0:512], self.KRd[:, kcols], QR[:, hg * 512:(hg + 1) * 512], False, True,
                            reads=[("KRd", jt // 4), kR(hg)])
                    P = self.P[it % NP]
                    S.add("act", lambda e: e.activation(P[:], banks[sbk][:, 0:512], AF.Exp, scale=SM_SCALE),
                          reads=[("ps", sbk)], writes=[("P", it % NP)])
                    if jt >= 16 * n:
                        eidx = (jt - 16 * n) - qb + 3
                        mk = self.maskE[:, eidx * 128:(eidx + 1) * 128].unsqueeze(1).broadcast_to([128, 4, 128])
                        P3 = P[:].rearrange("p (h q) -> p h q", h=4)
                        S.add("dve" if it % 2 == 0 else "pool", lambda e: e.tensor_tensor(P3, P3, mk, ALU.mult),
                              reads=[("P", it % NP), "maskE"], writes=[("P", it % NP)])
                    acc = self.rl[:, g * 512:(g + 1) * 512]

                    def acc_add(src, skey, first):
                        def emit_it():
                            if first:
                                S.add("dve", lambda e: e.tensor_copy(acc, src[:]), reads=[skey], writes=[("rl", g)])
                            else:
                                S.add("dve", lambda e: e.tensor_tensor(acc, acc, src[:], ALU.add),
                                      reads=[skey, ("rl", g)], writes=[("rl", g)])
                        pend.append((it + 4, emit_it))
                    if jt % 2 == 1:
                        pa = (it - 2) % NP
                        pi = (jt // 2 * 2 + g) % 4
                        PSb = self.PS[pi]
                        Pa = self.P[pa]
                        S.add("dve", lambda e: e.tensor_tensor(PSb[:], Pa[:], P[:], ALU.add),
                              reads=[("P", pa), ("P", it % NP)], writes=[("PS", pi)])
                        if jt % 4 == 3:
                            pj = ((jt // 2 - 1) * 2 + g) % 4
                            qi = (jt // 4 % 2) * 2 + g
                            PQb = self.PQ[qi]
                            PSa = self.PS[pj]
                            S.add("pool", lambda e: e.tensor_tensor(PQb[:], PSa[:], PSb[:], ALU.add),
                                  reads=[("PS", pj), ("PS", pi)], writes=[("PQ", qi)])
                            acc_add(PQb, ("PQ", qi), jt == 3)
                        elif jt >= 4 * nq4:
                            acc_add(PSb, ("PS", pi), False)
                    elif jt == Jq - 1 and jt >= 4 * nq4:
                        acc_add(P, ("P", it % NP), False)

                ob = 2 * (self.pass_idx % 2)
                self.pass_idx += 1

                def pv_part(it):
                    jt, g = its[it]
                    P = self.P[it % NP]
                    self.mm(ob + g, banks[ob + g][:, 0:512], self.V[:, jt * 128:(jt + 1) * 128], P[:], jt == 0,
                            jt == Jq - 1, reads=[("V", jt // 4), ("P", it % NP)])

                pend = []
                prev_tail = self.deferred
                self.deferred = []
                for it in range(nit + 2):
                    while prev_tail and prev_tail[0][0] <= it:
                        prev_tail.pop(0)[1]()
                    while pend and pend[0][0] <= it:
                        pend.pop(0)[1]()
                    if it < nit:
                        s_part(it)
                        tick_cnt[0] += 1
                        if tick_cnt[0] % tick_every == 0:
                            next(gen, None)
                    if it >= 2:
                        pv_part(it - 2)
                while prev_tail:
                    prev_tail.pop(0)[1]()
                while pend:
                    pend.pop(0)[1]()

                def mk_tail(g, ob=ob, ps_=ps_, qc=qc):
                    acc = self.rl[:, g * 512:(g + 1) * 512]
                    hi, lo = self.PQ[g], self.PQ[2 + g]
                    rc = self.sg[g]

                    def split():
                        S.add("dve", lambda e: e.tensor_copy(hi[:], acc), reads=[("rl", g)], writes=[("PQ", g)])
                        S.add("dve", lambda e: e.tensor_tensor(lo[:], acc, hi[:], ALU.subtract),
                              reads=[("rl", g), ("PQ", g)], writes=[("PQ", 2 + g)])

                    def finish():
                        self.mm(7, banks[7][:, 0:512], self.ones[:], hi[:], True, False, reads=["ones", ("PQ", g)])
                        self.mm(7, banks[7][:, 0:512], self.ones[:], lo[:], False, True, reads=["ones", ("PQ", 2 + g)])
                        S.add("dve", lambda e: e.reciprocal(rc[:], banks[7][:, 0:512]), reads=[("ps", 7)],
                              writes=[("sg", g)])
                        for hh in range(4):
                            h = 8 * ps_ + 4 * g + hh
                            S.add("dve", lambda e, hh=hh, h=h: e.tensor_tensor(
                                OT[:, h, qc], banks[ob + g][:, hh * 128:(hh + 1) * 128],
                                rc[:, hh * 128:(hh + 1) * 128], ALU.mult),
                                reads=[("ps", ob + g), ("sg", g)], writes=[("scr", h)])
                    return split, finish
                s0, f0 = mk_tail(0)
                s1, f1 = mk_tail(1)
                self.deferred = [(2, s0), (3, s1), (4, f0), (6, f1)]
            for _ in gen:
                pass
        while self.deferred:
            self.deferred.pop(0)[1]()
        bv, kv_ = self.fetch(base + 6, NH * 128)
        wv = bv[:, 0:NH * 128].rearrange("p (h d) -> p h d", h=NH)
        for h in range(NH):
            b = self.next_bank()
            self.mm(b, banks[b][:, 0:N], wv[:, h, :], OT[:, h, :], True, True, reads=[kv_, ("scr", h)])
            self.evac(OT[:, h, :], banks[b][:, 0:N], [("ps", b)], [("scr", h)])
        for dc in range(KC):
            bo_, ko = self.fetch(base + 7 + dc, NH * 128)
            wo = bo_[:, 0:NH * 128].rearrange("p (h d) -> p h d", h=NH)
            b = self.next_bank()
            for h in range(NH):
                self.mm(b, banks[b][:, 0:N], wo[:, h, :], OT[:, h, :], h == 0, h == NH - 1, reads=[ko, ("scr", h)])
            S.add("dve", lambda e, b=b, dc=dc: e.tensor_tensor(hT[:, dc, :], banks[b][:, 0:N], hT[:, dc, :], ALU.add),
                  reads=[("ps", b), ("h", dc)], writes=[("h", dc)])

    def phase_b(self, convert_slabs, rawwaits=()):
        S = self.S
        if rawwaits:
            S.add("sp", None, rawwaits=rawwaits)
        nch = self.nch
        io = self.io
        self.setup_common(nring=3, nsg=2)
        hT = self.hT
        self.phase_b_setup()
        hkeys = [("h", kc) for kc in range(KC)]
        if convert_slabs:
            stages = [hT[:].rearrange("p a b -> p (a b)"), self.scr[:, 0:8192].bitcast(F32)]
            skeys = [hkeys, [("scr", i) for i in range(16)]]
            outs = [self.ring[1][:], self.ring[2][:]]
            okeys = [[("slab", 1)], [("slab", 2)]]
            self.convert(convert_slabs, stages, skeys, outs, okeys)
        for n in range(nch):
            for kc in range(KC):
                self.dma("act" if kc % 2 else "sp", hT[:, kc, :], io["hA_in"][n][:, kc, :], reads=[], writes=[("h", kc)],
                         chan=self.tile_ch[kc])
            self.dma("sp", self.csQ[0:64, :], io["ropeK"][0, 0:64, n * TOK:(n + 1) * TOK], reads=[], writes=["csQ"])
            self.dma("sp", self.csQ[64:128, :], io["ropeK"][1, 64:128, n * TOK:(n + 1) * TOK], reads=[], writes=["csQ"])
            for l in (2, 3):
                self.ffn(l, "pre", TOK)
                self.attention(l - 2, n)
                self.ffn(l, "post", TOK)
            outs = [hT[:, kc, :] for kc in range(KC)]
            self.norm_h("final_norm", 0, TOK, out_fp32=(outs, hkeys))
            for kc in range(KC):
                self.dma("act" if kc % 2 else "sp", io["out"][n][:, kc, :], hT[:, kc, :], reads=[("h", kc)],
                         writes=[("out", n, kc)], chan=self.tile_ch[kc])

    def finish(self):
        lasts = [c.last for c in self.S.chans if c.last is not None]
        self.S.add("sp", None, extra=lasts)
        self.S.emit(self.nc, self.stack, self.semstack)


ALL_SLABS = list(range(NSLAB))


def build_program(nch, mode, slabs_in):
    nc = bass.Bass("TRN2", target_bir_lowering=False)
    stack = contextlib.ExitStack()
    B = Builder(nc, stack, nch)
    B.slab_map = {s: i for i, s in enumerate(slabs_in)}

    def dt(name, shape, dtp, kind):
        return nc.dram_tensor(name, list(shape), dtp, kind=kind).ap()

    io = {}
    io["wts"] = dt("wts", [len(slabs_in), 128, SLAB], F32, "ExternalInput")
    io["wbf"] = dt("wbf", [NSLAB, 128, SLAB], BF16, "Internal")
    io["consts"] = dt("consts", [128, NCONST], F32, "ExternalInput")
    io["ropeK"] = dt("ropeK", [2, 128, nch * TOK], F32, "ExternalInput")
    if mode == "A":
        io["xm"] = dt("xm", [nch, 128, KC, TOK], F32, "ExternalInput")
        io["xh"] = dt("xh", [128, KC, nch * HALO], F32, "ExternalInput")
        io["poolw"] = dt("poolw", [128, SLAB], F32, "ExternalInput")
        io["pcorr"] = dt("pcorr", [128, 4, 16], F32, "ExternalInput")
        io["ident"] = dt("ident", [128, 128], F32, "ExternalInput")
        io["hA"] = dt("hA", [nch, 128, KC, TOK], F32, "ExternalOutput")
        io["kvl"] = dt("kvl", [128, nch * TOK], BF16, "ExternalOutput")
        io["kvr"] = dt("kvr", [128, nch * TOK], BF16, "ExternalOutput")
        io["kvv"] = dt("kvv", [nch, 128, TOK], BF16, "ExternalOutput")
        B.io = io
        B.phase_a(slabs_in)
    else:
        io["hA_in"] = dt("hA_in", [nch, 128, KC, TOK], F32, "ExternalInput")
        io["kvl_g"] = dt("kvl_g", [4, 128, nch * TOK], BF16, "ExternalInput")
        io["kvr_g"] = dt("kvr_g", [4, 128, nch * TOK], BF16, "ExternalInput")
        io["kvv_g"] = dt("kvv_g", [4, nch, 128, TOK], BF16, "ExternalInput")
        io["maskE"] = dt("maskE", [128, 19 * 128], BF16, "ExternalInput")
        io["out"] = dt("out", [nch, 128, KC, TOK], F32, "ExternalOutput")
        B.io = io
        B.phase_b(slabs_in)
    B.finish()
    stack.close()
    return nc


def build_fused(nch):
    nc = bass.Bass("TRN2", target_bir_lowering=False)
    semstack = contextlib.ExitStack()

    def dt(name, shape, dtp, kind):
        return nc.dram_tensor(name, list(shape), dtp, kind=kind).ap()

    io = {}
    io["wts"] = dt("wts", [NSLAB, 128, SLAB], F32, "ExternalInput")
    io["wbf"] = dt("wbf", [NSLAB, 128, SLAB], BF16, "Internal")
    io["consts"] = dt("consts", [128, NCONST], F32, "ExternalInput")
    io["ropeK"] = dt("ropeK", [2, 128, nch * TOK], F32, "ExternalInput")
    io["xm"] = dt("xm", [nch, 128, KC, TOK], F32, "ExternalInput")
    io["xh"] = dt("xh", [128, KC, nch * HALO], F32, "ExternalInput")
    io["poolw"] = dt("poolw", [128, SLAB], F32, "ExternalInput")
    io["pcorr"] = dt("pcorr", [128, 4, 16], F32, "ExternalInput")
    io["ident"] = dt("ident", [128, 128], F32, "ExternalInput")
    io["maskE"] = dt("maskE", [128, 19 * 128], BF16, "ExternalInput")
    io["hA"] = dt("hA", [nch, 128, KC, TOK], F32, "Internal")
    kvl_t = nc.dram_tensor("kvl", [128, nch * TOK], BF16)
    kvr_t = nc.dram_tensor("kvr", [128, nch * TOK], BF16)
    kvv_t = nc.dram_tensor("kvv", [nch * 128, TOK], BF16)
    kvl_gt = nc.dram_tensor("kvl_g", [4 * 128, nch * TOK], BF16)
    kvr_gt = nc.dram_tensor("kvr_g", [4 * 128, nch * TOK], BF16)
    kvv_gt = nc.dram_tensor("kvv_g", [4 * nch * 128, TOK], BF16)
    io["kvl"] = kvl_t.ap()
    io["kvr"] = kvr_t.ap()
    io["kvv"] = kvv_t.ap().rearrange("(n p) t -> n p t", n=nch)
    kvl_g, kvr_g, kvv_g = kvl_gt.ap(), kvr_gt.ap(), kvv_gt.ap()
    io["out"] = dt("out", [nch, 128, KC, TOK], F32, "ExternalOutput")

    stackA = contextlib.ExitStack()
    A = Builder(nc, stackA, nch, tag="a_", semstack=semstack)
    A.slab_map = {s: s for s in ALL_SLABS}
    A.io = io
    A.phase_a(ALL_SLABS)
    A.finish()
    stackA.close()

    groups = [[0, 1, 2, 3], [4, 5, 6, 7]]
    csems = [semstack.enter_context(nc.semaphore("cc%d" % i)) for i in range(3)]
    with nc.Block() as cblk:
        @cblk.gpsimd
        def _(g):
            for i, (src, dst) in enumerate(((kvl_t, kvl_gt), (kvr_t, kvr_gt), (kvv_t, kvv_gt))):
                g.collective_compute("AllGather", mybir.AluOpType.bypass, replica_groups=groups,
                                     ins=[src.ap().opt()], outs=[dst.ap().opt()]).then_inc(csems[i])
            for i in range(3):
                g.wait_ge(csems[i], 1)

    ioB = dict(io)
    ioB["hA_in"] = io["hA"]
    ioB["kvl_g"] = kvl_g.rearrange("(r p) t -> r p t", r=4)
    ioB["kvr_g"] = kvr_g.rearrange("(r p) t -> r p t", r=4)
    ioB["kvv_g"] = kvv_g.rearrange("(r n p) t -> r n p t", r=4, n=nch)
    stackB = contextlib.ExitStack()
    Bb = Builder(nc, stackB, nch, tag="b_", semstack=semstack)
    Bb.slab_map = {s: s for s in ALL_SLABS}
    Bb.io = ioB
    Bb.phase_b([])
    Bb.finish()
    stackB.close()
    semstack.close()
    return nc


def to_fm(xt):
    T = xt.shape[0]
    return np.ascontiguousarray(xt.T.reshape(KC, 128, T).transpose(1, 0, 2))


def from_fm(a):
    T = a.shape[2]
    return a.transpose(2, 1, 0).reshape(T, D)


def core_inputs(x, nch, W, consts, poolw, cc, ss):
    kk = np.arange(128)[:, None]
    qq = np.arange(128)[None, :]
    ident = np.eye(128, dtype=np.float32)
    ins = []
    for core in range(8):
        b, c = core // 4, core % 4
        xm = np.zeros((nch, 128, KC, TOK), np.float32)
        xh = np.zeros((128, KC, nch * HALO), np.float32)
        pos = np.zeros(nch * TOK, np.int64)
        for n in range(nch):
            ci = 4 * n + c
            xm[n] = to_fm(x[b, ci * TOK:(ci + 1) * TOK])
            if ci > 0:
                xh[:, :, n * HALO:(n + 1) * HALO] = to_fm(x[b, ci * TOK - HALO:ci * TOK])
            pos[n * TOK:(n + 1) * TOK] = np.arange(ci * TOK, (ci + 1) * TOK)
        ropeK = np.ascontiguousarray(np.stack([cc[:, pos], ss[:, pos]], 0))
        pcorr = np.ones((128, 4, 16), np.float32)
        if c == 0:
            for g, w in enumerate(POOL_W):
                pcorr[:, g, :] = (w / np.minimum(np.arange(16) + 1.0, w)).astype(np.float32)[None, :]
        maskE = np.zeros((128, 19, 128), np.float32)
        for e in range(-3, 16):
            maskE[:, e + 3, :] = (((e - 4 * c) * 128 + kk) <= qq)
        maskE = maskE.reshape(128, 19 * 128).astype(ml_dtypes.bfloat16)
        ins.append({"wts": W, "consts": consts, "ropeK": ropeK, "xm": xm, "xh": xh, "poolw": poolw,
                    "pcorr": pcorr, "ident": ident, "maskE": maskE})
    return ins


def kernel(**inp):
    x = np.asarray(inp["x"], np.float32)
    Bn, S, _ = x.shape
    nch = S // (4 * TOK)
    assert Bn == 2 and S == 4 * nch * TOK
    W = host_slabs(inp)
    consts = host_consts(inp)
    poolw = host_poolw(inp)
    cc, ss = rope_tables_host(S)
    ins = core_inputs(x, nch, W, consts, poolw, cc, ss)
    nc = build_fused(nch)
    res = run_bass_kernel_spmd(nc, ins, core_ids=list(range(8))).results
    out = np.zeros((Bn, S, D), np.float32)
    for core in range(8):
        b, c = core // 4, core % 4
        o = np.asarray(res[core]["out"])
        for n in range(nch):
            ci = 4 * n + c
            out[b, ci * TOK:(ci + 1) * TOK] = from_fm(o[n])
    return out
```

```python
import contextlib
import numpy as np
import ml_dtypes
import concourse.bass as bass
import concourse.mybir as mybir
from concourse.bass_utils import run_bass_kernel_spmd

F32 = mybir.dt.float32
BF16 = mybir.dt.bfloat16
AF = mybir.ActivationFunctionType
ALU = mybir.AluOpType

D = 1024
KC = 8
F = 2816
FC = 22
NH = 16
EPS = 1e-6
TOK = 512
HALO = 32
SLAB = 4096
POOL_W = (2, 4, 8, 16)
SM_SCALE = 192.0 ** -0.5


class Op:
    __slots__ = ("eng", "fn", "deps", "signal", "sigval", "chan", "idx", "rawwaits")


class Chan:
    def __init__(self, name):
        self.name = name
        self.sem = None
        self.count = 0
        self.last = None


class Sched:
    ENGS = ("pe", "act", "dve", "pool", "sp")
    BASS = {"pe": "tensor", "act": "scalar", "dve": "vector", "pool": "gpsimd", "sp": "sync"}

    def __init__(self, tag=""):
        self.tag = tag
        self.streams = {e: [] for e in self.ENGS}
        self.last_w = {}
        self.readers = {}
        self.chans = []

    def chan(self, name):
        c = Chan(name)
        self.chans.append(c)
        return c

    def add(self, eng, fn, reads=(), writes=(), chan=None, extra=(), rawwaits=()):
        op = Op()
        op.rawwaits = rawwaits
        op.eng, op.fn, op.chan = eng, fn, chan
        op.signal = chan is not None
        op.sigval = None
        op.idx = len(self.streams[eng])
        deps = {}
        for k in reads:
            w = self.last_w.get(k)
            if w is not None:
                deps[id(w)] = w
        for k in writes:
            w = self.last_w.get(k)
            if w is not None:
                deps[id(w)] = w
            for r in self.readers.get(k, ()):
                deps[id(r)] = r
        for w in extra:
            if w is not None:
                deps[id(w)] = w
        if chan is not None and chan.last is not None:
            deps[id(chan.last)] = chan.last
        dl = []
        for w in deps.values():
            if w is op:
                continue
            if w.chan is None and w.eng == "pe" and eng == "pe" and chan is None:
                continue
            dl.append(w)
        op.deps = dl
        for k in writes:
            self.last_w[k] = op
            self.readers[k] = []
        for k in reads:
            lst = self.readers.setdefault(k, [])
            if chan is None:
                lst[:] = [r for r in lst if not (r.chan is None and r.eng == eng)]
            lst.append(op)
        if chan is not None:
            chan.last = op
            chan.count += 16
            op.sigval = chan.count
        self.streams[eng].append(op)
        return op

    def emit(self, nc, stack, semstack=None):
        semstack = semstack or stack
        for e in self.ENGS:
            for op in self.streams[e]:
                for w in op.deps:
                    w.signal = True
        esem = {}
        for e in self.ENGS:
            esem[e] = semstack.enter_context(nc.semaphore(self.tag + "s_" + e))
            n = 0
            for op in self.streams[e]:
                if op.chan is None and op.signal:
                    n += 1
                    op.sigval = n
        for c in self.chans:
            c.sem = semstack.enter_context(nc.semaphore(self.tag + "c_" + c.name))
        block = stack.enter_context(nc.Block())
        for e in self.ENGS:
            ops = self.streams[e]
            if not ops:
                continue

            def body(eng, ops=ops):
                known = {}
                for op in ops:
                    for (rs, rv) in op.rawwaits:
                        eng.wait_ge(rs, rv)
                    for w in op.deps:
                        sem = w.chan.sem if w.chan is not None else esem[w.eng]
                        key = id(sem)
                        if known.get(key, 0) < w.sigval:
                            eng.wait_ge(sem, w.sigval)
                            known[key] = w.sigval
                    if op.fn is None:
                        continue
                    inst = op.fn(eng)
                    if op.chan is not None:
                        inst.then_inc(op.chan.sem, 16)
                    elif op.signal:
                        inst.then_inc(esem[op.eng], 1)

            getattr(block, self.BASS[e])(body)


def _consts_layout():
    off = {}
    c = 0
    for name, n in (("ffn_pre_norm", 32), ("mix_norm", 32), ("ffn_post_norm", 32),
                    ("pool_scale", 16), ("kv_in_norm", 8), ("final_norm", 8),
                    ("ckv_norm", 1), ("q_lora_norm", 4)):
        off[name] = c
        c += n
    return off, c


COFF, NCONST = _consts_layout()


def _slab_plan():
    plan = {}
    n = 0
    for l in range(4):
        for which in ("pre", "post"):
            plan[("ffn", l, which)] = n
            n += 19
    for j in range(2):
        plan[("attn", j)] = n
        n += 15
    plan["misc"] = n
    n += 1
    return plan, n


PLAN, NSLAB = _slab_plan()
A_SLABS = [s for l in (0, 1) for w in ("pre", "post") for s in range(PLAN[("ffn", l, w)], PLAN[("ffn", l, w)] + 19)] + [PLAN["misc"]]
B_SLABS = [s for l in (2, 3) for w in ("pre", "post") for s in range(PLAN[("ffn", l, w)], PLAN[("ffn", l, w)] + 19)] + \
          [s for j in (0, 1) for s in range(PLAN[("attn", j)], PLAN[("attn", j)] + 15)]


def host_slabs(inp):
    W = np.zeros((NSLAB, 128, SLAB), np.float32)
    for l in range(4):
        for which in ("pre", "post"):
            b = PLAN[("ffn", l, which)]
            if which == "pre":
                wg_, wu_, wd_ = inp["ffn_pre_wg"], inp["ffn_pre_wu"], inp["ffn_pre_wd"]
            else:
                wg_, wu_, wd_ = inp["ffn_post_wg"], inp["ffn_post_wu"], inp["ffn_post_wd"]
            wg, wu, wd = np.asarray(wg_[l]), np.asarray(wu_[l]), np.asarray(wd_[l])
            gu = np.stack([wg, wu], 0).reshape(2, KC, 128, FC, 128)
            gu = gu.transpose(3, 2, 0, 1, 4)
            gu = gu.reshape(11, 2, 128, 2, KC, 128).transpose(0, 2, 1, 3, 4, 5)
            W[b:b + 11] = gu.reshape(11, 128, SLAB)
            w2 = wd.reshape(FC, 128, KC, 128).transpose(2, 1, 0, 3)
            W[b + 11:b + 19, :, :FC * 128] = w2.reshape(8, 128, FC * 128)
    wuk = np.asarray(inp["w_uk"])
    wuv = np.asarray(inp["w_uv"])
    for j in range(2):
        b = PLAN[("attn", j)]
        wdq = np.asarray(inp["w_dq"][j]).reshape(KC, 128, 256).transpose(1, 0, 2)
        W[b, :, :KC * 256] = wdq.reshape(128, KC * 256)
        wuq = np.asarray(inp["w_uq"][j]).reshape(2, 128, NH, 192).transpose(1, 0, 2, 3)
        nope = wuq[..., :128]
        W[b + 1, :, :NH * 128] = nope[:, 0].reshape(128, NH * 128)
        W[b + 2, :, :NH * 128] = nope[:, 1].reshape(128, NH * 128)
        rope = wuq[..., 128:]
        swap = np.concatenate([rope[..., 32:], rope[..., :32]], -1)
        rs = np.concatenate([rope, swap], -1)
        W[b + 3, :, :NH * 128] = rs[:, 0].reshape(128, NH * 128)
        W[b + 4, :, :NH * 128] = rs[:, 1].reshape(128, NH * 128)
        W[b + 5, :, :NH * 128] = wuk.transpose(2, 1, 0).reshape(128, NH * 128)
        W[b + 6, :, :NH * 128] = wuv.reshape(128, NH * 128)
        wo = np.asarray(inp["w_o"][j]).reshape(NH, 128, KC, 128).transpose(2, 1, 0, 3)
        W[b + 7:b + 15, :, :NH * 128] = wo.reshape(8, 128, NH * 128)
    b = PLAN["misc"]
    wdkv = np.asarray(inp["w_dkv"]).reshape(KC, 128, 192).transpose(1, 0, 2)
    lat = wdkv[..., :128]
    rope = wdkv[..., 128:]
    swap = np.concatenate([rope[..., 32:], rope[..., :32]], -1)
    blk = np.concatenate([lat, rope, rope, swap, swap], -1)
    W[b, :, :KC * 384] = blk.reshape(128, KC * 384)
    return W


def host_consts(inp):
    C = np.zeros((128, NCONST), np.float32)

    def put(name, arr):
        a = np.asarray(arr, np.float32).reshape(-1, 128).T
        C[:, COFF[name]:COFF[name] + a.shape[1]] = a
    put("ffn_pre_norm", inp["ffn_pre_norm"])
    put("mix_norm", inp["mix_norm"])
    put("ffn_post_norm", inp["ffn_post_norm"])
    put("pool_scale", inp["pool_scale"])
    put("kv_in_norm", inp["kv_in_norm"])
    put("final_norm", inp["final_norm"])
    put("ckv_norm", inp["ckv_norm"])
    put("q_lora_norm", inp["q_lora_norm"])
    return C


def host_poolw(inp):
    pw = np.asarray(inp["pool_w"], np.float32)
    pw = pw.reshape(2, 4, 2, 128, 256).transpose(3, 0, 1, 2, 4)
    return np.ascontiguousarray(pw.reshape(128, 2 * 4 * 2 * 256))


def rope_tables_host(seq):
    pos = np.arange(seq, dtype=np.float64)
    inv = 10000.0 ** (-np.arange(0, 64, 2, dtype=np.float64) / 64.0)
    ang = pos[None, :] * inv[:, None]
    cos = np.cos(ang)
    sin = np.sin(ang)
    c64 = np.concatenate([cos, cos], 0)
    s64 = np.concatenate([-sin, sin], 0)
    cc = np.concatenate([c64, c64], 0).astype(np.float32)
    ss = np.concatenate([s64, s64], 0).astype(np.float32)
    return cc, ss


class Builder:
    def __init__(self, nc, stack, nch, tag="", semstack=None):
        self.nc = nc
        self.stack = stack
        self.semstack = semstack
        self.tag = tag
        self.S = Sched(tag)
        self.nch = nch
        self.psum_rr = 0
        self.ring_cnt = 0
        self.sg_cnt = 0
        self.cp_cnt = 0
        self.bg_on = False
        self.pass_idx = 0
        self.deferred = []

    def sb(self, name, shape, dt):
        return self.stack.enter_context(self.nc.sbuf_tensor(self.tag + "sb_" + name, shape, dt))

    def setup_common(self, nring, nsg=3):
        nc = self.nc
        self.banks = [self.stack.enter_context(nc.psum_tensor(self.tag + "ps%d" % i, [128, 512], F32)) for i in range(8)]
        self.ring = [self.sb("slab%d" % i, [128, SLAB], BF16) for i in range(nring)]
        self.ring_ch = [self.S.chan("ring%d" % i) for i in range(nring)]
        self.sg = [self.sb("sg%d" % i, [128, TOK], F32) for i in range(nsg)]
        self.ones = self.sb("ones", [128, 128], BF16)
        self.consts = self.sb("consts", [128, NCONST], F32)
        self.hT = self.sb("hT", [128, KC, TOK], F32)
        self.uT = self.sb("uT", [128, KC, TOK], BF16)
        self.scr = self.sb("scr", [128, 12288], BF16)
        self.AT = self.scr[:, 0:FC * TOK].rearrange("p (c t) -> p c t", c=FC)
        self.misc_ch = [self.S.chan("misc%d" % i) for i in range(6)]
        self.tile_ch = [self.S.chan("tile%d" % i) for i in range(KC)]
        self.misc_i = 0
        S = self.S
        S.add("dve", lambda e: e.memset(self.ones[:], 1.0), writes=["ones"])
        self.dma("sp", self.consts[:], self.io["consts"], reads=[], writes=["consts"])

    def mch(self):
        self.misc_i += 1
        return self.misc_ch[self.misc_i % len(self.misc_ch)]

    def dma(self, q, out, in_, reads, writes, chan=None):
        chan = chan or self.mch()
        return self.S.add(q, lambda e: e.dma_start(out=out, in_=in_), reads=reads, writes=writes, chan=chan)

    def next_bank(self, allowed=range(8)):
        allowed = list(allowed)
        self.psum_rr += 1
        return allowed[self.psum_rr % len(allowed)]

    def mm(self, bank, out, lhsT, rhs, start, stop, reads):
        self.S.add("pe", lambda e: e.matmul(out, lhsT, rhs, start=start, stop=stop),
                   reads=reads, writes=[("ps", bank)])

    def convert(self, slabs, stages, stage_keys, outs, out_keys):
        S = self.S
        wts, wbf = self.io["wts"], self.io["wbf"]
        ns, no = len(stages), len(outs)
        ld_ch = [S.chan("cvl%d" % i) for i in range(ns)]
        st_ch = [S.chan("cvs%d" % i) for i in range(no)]
        engs = ["pool", "act", "dve"]
        n = len(slabs)
        depth = ns - 1
        for i in range(n + depth):
            if i < n:
                s = slabs[i]
                self.dma("sp", stages[i % ns], wts[self.slab_map[s]], reads=[], writes=stage_keys[i % ns],
                         chan=ld_ch[i % ns])
            j = i - depth
            if j >= 0:
                s = slabs[j]
                st, sk = stages[j % ns], stage_keys[j % ns]
                ob, ok = outs[j % no], out_keys[j % no]
                eng = engs[j % 3]
                if eng == "act":
                    S.add("act", lambda e, ob=ob, st=st: e.activation(ob, st, AF.Copy), reads=sk, writes=ok)
                else:
                    S.add(eng, lambda e, ob=ob, st=st: e.tensor_copy(ob, st), reads=sk, writes=ok)
                self.dma("sp", wbf[s], ob, reads=ok, writes=[("wbf", s)], chan=st_ch[j % no])

    def bg_setup(self, slabs):
        pieces = []
        for s in slabs:
            used = SLAB
            for (key, base) in PLAN.items():
                if key == "misc":
                    continue
                if key[0] == "ffn" and base + 11 <= s < base + 19:
                    used = FC * 128
                if key[0] == "attn" and base <= s < base + 15:
                    used = 2048
            o = 0
            while o < used:
                ln = min(2048, used - o)
                pieces.append((s, o, ln))
                o += ln
        self.bg_pieces = pieces
        self.bg_i = 0
        self.bg_stage = [self.sb("bgs%d" % i, [128, 2048], F32) for i in range(2)]
        self.bg_out = [self.sb("bgo%d" % i, [128, 2048], BF16) for i in range(2)]
        self.bg_lch = [self.S.chan("bgl%d" % i) for i in range(2)]
        self.bg_sch = [self.S.chan("bgs%d" % i) for i in range(2)]

    def bg_tick(self):
        pcs = getattr(self, "bg_pieces", None)
        if not pcs:
            return
        i = self.bg_i
        if i > len(pcs):
            return
        self.bg_i += 1
        S = self.S
        wts, wbf = self.io["wts"], self.io["wbf"]
        if i < len(pcs):
            s, o, ln = pcs[i]
            self.dma("act", self.bg_stage[i % 2][:, 0:ln], wts[self.slab_map[s]][:, o:o + ln], reads=[],
                     writes=[("bgs", i % 2)], chan=self.bg_lch[i % 2])
        j = i - 1
        if j >= 0:
            s, o, ln = pcs[j]
            st, ob = self.bg_stage[j % 2], self.bg_out[j % 2]
            S.add("act", lambda e: e.activation(ob[:, 0:ln], st[:, 0:ln], AF.Copy),
                  reads=[("bgs", j % 2)], writes=[("bgo", j % 2)])
            self.dma("act", wbf[s][:, o:o + ln], ob[:, 0:ln], reads=[("bgo", j % 2)], writes=[("wbf", s)],
                     chan=self.bg_sch[j % 2])

    def fetch(self, s, nelem=SLAB):
        if self.bg_on:
            self.bg_tick()
        i = self.ring_cnt % len(self.ring)
        self.ring_cnt += 1
        buf = self.ring[i]
        self.dma("sp", buf[:, 0:nelem], self.io["wbf"][s][:, 0:nelem], reads=[("wbf", s)],
                 writes=[("slab", i)], chan=self.ring_ch[i])
        return buf, ("slab", i)

    def norm(self, h_aps, h_keys, gains, out_aps, out_keys, N, dim, sq_src=None, sq_keys=None):
        S = self.S
        n = len(h_aps)
        sq_src = sq_src or h_aps
        sq_keys = sq_keys or h_keys
        hsq = self.scr[:, 0:8 * TOK].rearrange("p (c t) -> p c t", c=8)
        for i in range(n):
            S.add("act", lambda e, i=i: e.activation(hsq[:, i, 0:N], sq_src[i], AF.Square),
                  reads=[sq_keys[i]], writes=[("scr", i)])
        b = self.next_bank()
        bk = self.banks[b]
        for i in range(n):
            self.mm(b, bk[:, 0:N], self.ones[:], hsq[:, i, 0:N], i == 0, i == n - 1, reads=["ones", ("scr", i)])
        S.add("act", lambda e: e.activation(bk[:, 0:N], bk[:, 0:N], AF.Sqrt, bias=self.eps[:], scale=1.0 / dim),
              reads=[("ps", b), "eps"], writes=[("ps", b)])
        S.add("dve", lambda e: e.reciprocal(bk[:, 0:N], bk[:, 0:N]), reads=[("ps", b)], writes=[("ps", b)])
        for i in range(n):
            S.add("dve", lambda e, i=i: e.scalar_tensor_tensor(out_aps[i], h_aps[i], gains[i], bk[:, 0:N],
                                                               ALU.mult, ALU.mult),
                  reads=[h_keys[i], ("ps", b), "consts"], writes=[out_keys[i]])

    def cgain(self, name, idx):
        c = COFF[name] + idx
        return self.consts[:, c:c + 1]

    def norm_h(self, gname, gbase, N, out_fp32=None):
        hT, uT = self.hT, self.uT
        h_aps = [hT[:, kc, 0:N] for kc in range(KC)]
        h_keys = [("h", kc) for kc in range(KC)]
        gains = [self.cgain(gname, gbase + kc) for kc in range(KC)]
        if out_fp32 is None:
            outs = [uT[:, kc, 0:N] for kc in range(KC)]
            okeys = [("u", kc) for kc in range(KC)]
        else:
            outs, okeys = out_fp32
        self.norm(h_aps, h_keys, gains, outs, okeys, N, float(D))

    def ffn(self, l, which, N):
        S = self.S
        hT, uT, AT = self.hT, self.uT, self.AT
        self.norm_h("ffn_%s_norm" % which, l * 8, N)
        base = PLAN[("ffn", l, which)]
        for s in range(11):
            buf, skey = self.fetch(base + s)
            v = buf[:].rearrange("p (a g k f) -> p a g k f", a=2, g=2, k=KC)
            for fcl in range(2):
                fc = 2 * s + fcl
                bg = self.next_bank()
                for kc in range(KC):
                    self.mm(bg, self.banks[bg][:, 0:N], v[:, fcl, 0, kc, :], uT[:, kc, 0:N], kc == 0, kc == KC - 1,
                            reads=[skey, ("u", kc)])
                bu = self.next_bank()
                for kc in range(KC):
                    self.mm(bu, self.banks[bu][:, 0:N], v[:, fcl, 1, kc, :], uT[:, kc, 0:N], kc == 0, kc == KC - 1,
                            reads=[skey, ("u", kc)])
                self.sg_cnt += 1
                si = self.sg_cnt % len(self.sg)
                sg = self.sg[si]
                S.add("act", lambda e, bg=bg, sg=sg: e.activation(sg[:, 0:N], self.banks[bg][:, 0:N], AF.Silu),
                      reads=[("ps", bg)], writes=[("sg", si)])
                S.add("dve", lambda e, bu=bu, sg=sg, fc=fc: e.tensor_tensor(AT[:, fc, 0:N], self.banks[bu][:, 0:N],
                                                                            sg[:, 0:N], ALU.mult),
                      reads=[("ps", bu), ("sg", si)], writes=[("scr", fc)])
        for dc in range(KC):
            buf, skey = self.fetch(base + 11 + dc, FC * 128)
            v = buf[:, 0:FC * 128].rearrange("p (c d) -> p c d", c=FC)
            bo = self.next_bank()
            for fc in range(FC):
                self.mm(bo, self.banks[bo][:, 0:N], v[:, fc, :], AT[:, fc, 0:N], fc == 0, fc == FC - 1,
                        reads=[skey, ("scr", fc)])
            S.add("dve", lambda e, bo=bo, dc=dc: e.scalar_tensor_tensor(hT[:, dc, 0:N], self.banks[bo][:, 0:N], 0.5,
                                                                        hT[:, dc, 0:N], ALU.mult, ALU.add),
                  reads=[("ps", bo), ("h", dc)], writes=[("h", dc)])

    def phase_a_setup(self):
        nch = self.nch
        self.PADW = 16
        self.ub = self.sb("ub", [128, KC, 16 + TOK], F32)
        self.tmp = [self.sb("ptmp%d" % i, [128, 2, 16 + TOK], F32) for i in range(4)]
        self.yT = self.sb("yT", [128, KC, TOK], BF16)
        self.halo_u = self.sb("halo_u", [128, 2, KC, nch * HALO], F32)
        self.poolw = self.sb("poolw", [128, 2 * 4 * 2 * 256], BF16)
        self.pcorr = self.sb("pcorr", [128, 4, 16], F32)
        self.wdkv = self.sb("wdkv", [128, KC * 384], BF16)
        self.cc = self.sb("cc", [128, TOK], F32)
        self.ss = self.sb("ss", [128, TOK], F32)
        self.lat = self.sb("lat", [128, TOK], F32)
        self.t1 = self.sb("t1", [128, TOK], F32)
        self.t2 = self.sb("t2", [128, TOK], F32)
        self.kvl_sb = self.sb("kvl_sb", [128, TOK], BF16)
        self.kvr_sb = self.sb("kvr_sb", [128, TOK], BF16)
        self.kvv_sb = self.sb("kvv_sb", [128, TOK], BF16)
        self.ident = self.sb("ident", [128, 128], BF16)
        self.eps = self.sb("eps", [128, 1], F32)
        S = self.S
        S.add("dve", lambda e: e.memset(self.eps[:], EPS), writes=["eps"])
        ubflat = self.ub[:].rearrange("p a b -> p (a b)")[:, 0:SLAB]
        ubkeys = [("ub", kc) for kc in range(KC)] + [("ubpad", 0)]
        self.dma("sp", ubflat, self.io["poolw"], reads=[], writes=ubkeys)
        S.add("dve", lambda e: e.tensor_copy(self.poolw[:], ubflat), reads=ubkeys, writes=["poolw"])
        self.dma("sp", self.pcorr[:], self.io["pcorr"], reads=[], writes=["pcorr"])
        stg2 = self.sb("stgI", [128, 128], F32)
        self.dma("sp", stg2[:], self.io["ident"], reads=[], writes=["stgI"])
        S.add("dve", lambda e: e.tensor_copy(self.ident[:], stg2[:]), reads=["stgI"], writes=["ident"])

    def pool_mix(self, l, N, n, is_halo):
        S = self.S
        ub, yT, hT = self.ub, self.yT, self.hT
        W = 16 + N
        outs = [ub[:, kc, 16:16 + N] for kc in range(KC)]
        okeys = [("ub", kc) for kc in range(KC)]
        self.norm_h("mix_norm", l * 8, N, out_fp32=(outs, okeys))
        if is_halo:
            for kc in range(KC):
                S.add("pool", lambda e, kc=kc: e.tensor_copy(self.halo_u[:, l, kc, 0:N], ub[:, kc, 16:16 + N]),
                      reads=[("ub", kc)], writes=[("halo_u", l)])
            S.add("pool", lambda e: e.memset(ub[:, :, 0:16], 0.0), writes=[("ubpad", 0)])
            if l == 1:
                return
        else:
            S.add("pool", lambda e: e.tensor_copy(ub[:, :, 0:16], self.halo_u[:, l, :, n * HALO + 16:n * HALO + 32]),
                  reads=[("halo_u", l)], writes=[("ubpad", 0)])
        for g in range(4):
            eng = "dve" if g in (0, 3) else "pool"
            ta, tb = (self.tmp[0], self.tmp[1]) if eng == "dve" else (self.tmp[2], self.tmp[3])
            ka, kb = ("tmp", 0 if eng == "dve" else 2), ("tmp", 1 if eng == "dve" else 3)
            src, skeys = ub[:, 2 * g:2 * g + 2, :], [("ub", 2 * g), ("ub", 2 * g + 1), ("ubpad", 0)]
            lo = 0
            for lev in range(g + 1):
                sh = 1 << lev
                nlo = lo + sh
                dst, dkey = (ta, ka) if lev % 2 == 0 else (tb, kb)
                S.add(eng, lambda e, dst=dst, src=src, nlo=nlo, sh=sh: e.tensor_tensor(
                    dst[:, :, nlo:W], src[:, :, nlo:W], src[:, :, nlo - sh:W - sh], ALU.add),
                    reads=skeys, writes=[dkey])
                src, skeys, lo = dst, [dkey], nlo
            if n == 0 and not is_halo:
                for j in range(2):
                    S.add("dve", lambda e, src=src, j=j, g=g: e.tensor_tensor(
                        src[:, j, 16:32], src[:, j, 16:32], self.pcorr[:, g, :], ALU.mult),
                        reads=skeys + ["pcorr"], writes=skeys)
            for j in range(2):
                kc = 2 * g + j
                S.add("dve", lambda e, src=src, j=j, kc=kc, g=g: e.scalar_tensor_tensor(
                    yT[:, kc, 0:N], src[:, j, 16:W], 1.0 / POOL_W[g], ub[:, kc, 16:W], ALU.mult, ALU.subtract),
                    reads=skeys + [("ub", kc)], writes=[("y", kc)])
        pw = self.poolw[:].rearrange("p (l g i d) -> p l g i d", l=2, g=4, i=2)
        for g in range(4):
            for oc in range(2):
                b = self.next_bank()
                for ic in range(2):
                    self.mm(b, self.banks[b][:, 0:N], pw[:, l, g, ic, oc * 128:(oc + 1) * 128], yT[:, 2 * g + ic, 0:N],
                            ic == 0, ic == 1, reads=["poolw", ("y", 2 * g + ic)])
                kc = 2 * g + oc
                sc = self.cgain("pool_scale", l * 8 + kc)
                S.add("dve", lambda e, b=b, kc=kc, sc=sc: e.scalar_tensor_tensor(
                    hT[:, kc, 0:N], self.banks[b][:, 0:N], sc, hT[:, kc, 0:N], ALU.mult, ALU.add),
                    reads=[("ps", b), ("h", kc), "consts"], writes=[("h", kc)])

    def kv_latent(self, n):
        S = self.S
        N = TOK
        uT = self.uT
        io = self.io
        self.norm_h("kv_in_norm", 0, N)
        wd = self.wdkv[:].rearrange("p (k c) -> p k c", k=KC)
        bl, br, bs = self.next_bank(), self.next_bank(), self.next_bank()
        for (b, c0) in ((bl, 0), (br, 128), (bs, 256)):
            for kc in range(KC):
                self.mm(b, self.banks[b][:, 0:N], wd[:, kc, c0:c0 + 128], uT[:, kc, 0:N], kc == 0, kc == KC - 1,
                        reads=["wdkv", ("u", kc)])
        S.add("act", lambda e: e.activation(self.lat[:], self.banks[bl][:, 0:N], AF.Copy),
              reads=[("ps", bl)], writes=["lat"])
        self.norm([self.lat[:]], ["lat"], [self.cgain("ckv_norm", 0)], [self.kvl_sb[:]], ["kvl_sb"], N, 128.0)
        self.dma("sp", self.cc[:], io["ropeK"][0, :, n * TOK:(n + 1) * TOK], reads=[], writes=["cc"])
        self.dma("sp", self.ss[:], io["ropeK"][1, :, n * TOK:(n + 1) * TOK], reads=[], writes=["ss"])
        S.add("dve", lambda e: e.tensor_tensor(self.t1[:], self.banks[br][:, 0:N], self.cc[:], ALU.mult),
              reads=[("ps", br), "cc"], writes=["t1"])
        S.add("dve", lambda e: e.tensor_tensor(self.t2[:], self.banks[bs][:, 0:N], self.ss[:], ALU.mult),
              reads=[("ps", bs), "ss"], writes=["t2"])
        S.add("pool", lambda e: e.tensor_tensor(self.kvr_sb[:], self.t1[:], self.t2[:], ALU.add),
              reads=["t1", "t2"], writes=["kvr_sb"])
        bt = self.next_bank()
        btv = self.banks[bt][:].bitcast(BF16)
        for i in range(4):
            S.add("pe", lambda e, i=i: e.transpose(btv[:, i * 128:(i + 1) * 128], self.kvl_sb[:, i * 128:(i + 1) * 128],
                                                   self.ident[:]),
                  reads=["kvl_sb", "ident"], writes=[("ps", bt)])
        S.add("act", lambda e: e.activation(self.kvv_sb[:], btv[:, 0:512], AF.Copy), reads=[("ps", bt)], writes=["kvv_sb"])
        self.dma("sp", io["kvl"][:, n * TOK:(n + 1) * TOK], self.kvl_sb[:], reads=["kvl_sb"], writes=[("kvl", n)])
        self.dma("sp", io["kvr"][:, n * TOK:(n + 1) * TOK], self.kvr_sb[:], reads=["kvr_sb"], writes=[("kvr", n)])
        self.dma("sp", io["kvv"][n], self.kvv_sb[:], reads=["kvv_sb"], writes=[("kvv", n)])

    def phase_a(self, convert_slabs):
        S = self.S
        nch = self.nch
        io = self.io
        self.setup_common(nring=4)
        hT = self.hT
        self.phase_a_setup()
        hkeys = [("h", kc) for kc in range(KC)]
        ubflat = self.ub[:].rearrange("p a b -> p (a b)")[:, 0:SLAB]
        ubkeys = [("ub", kc) for kc in range(KC)] + [("ubpad", 0)]
        stages = [ubflat, hT[:].rearrange("p a b -> p (a b)"), self.scr[:, 0:8192].bitcast(F32)]
        skeys = [ubkeys, hkeys, [("scr", i) for i in range(16)]]
        now = [s_ for s_ in convert_slabs if s_ in A_SLABS]
        later = [s_ for s_ in convert_slabs if s_ not in A_SLABS]
        self.convert(now, stages, skeys, [r[:] for r in self.ring], [[("slab", i)] for i in range(len(self.ring))])
        if later:
            self.bg_setup(later)
        b = PLAN["misc"]
        self.dma("sp", self.wdkv[:], io["wbf"][b][:, 0:KC * 384], reads=[("wbf", b)], writes=["wdkv"])
        NHc = nch * HALO
        hkeys = [("h", kc) for kc in range(KC)]
        self.dma("sp", hT[:, :, 0:NHc], io["xh"], reads=[], writes=hkeys)
        self.ffn(0, "pre", NHc)
        self.pool_mix(0, NHc, 0, True)
        self.ffn(0, "post", NHc)
        self.ffn(1, "pre", NHc)
        self.pool_mix(1, NHc, 0, True)
        self.bg_on = True
        for n in range(nch):
            for kc in range(KC):
                self.dma("act" if kc % 2 else "sp", hT[:, kc, :], io["xm"][n][:, kc, :], reads=[], writes=[("h", kc)],
                         chan=self.tile_ch[kc])
            for l in (0, 1):
                self.ffn(l, "pre", TOK)
                self.pool_mix(l, TOK, n, False)
                self.ffn(l, "post", TOK)
            self.kv_latent(n)
            for kc in range(KC):
                self.dma("act" if kc % 2 else "sp", io["hA"][n][:, kc, :], hT[:, kc, :], reads=[("h", kc)],
                         writes=[("hA", n, kc)], chan=self.tile_ch[kc])
        while getattr(self, "bg_pieces", None) and self.bg_i <= len(self.bg_pieces):
            self.bg_tick()
        self.bg_on = False

    def phase_b_setup(self):
        nch = self.nch
        NKT = 16 * nch
        self.NKT = NKT
        self.KTl = self.sb("KTl", [128, NKT * 128], BF16)
        self.KRd = self.sb("KRd", [128, NKT * 128], BF16)
        self.V = self.sb("V", [128, NKT * 128], BF16)
        self.cqn = self.sb("cqn", [128, 2, TOK], BF16)
        self.P = [self.sb("P%d" % i, [128, 512], BF16) for i in range(5)]
        self.PS = [self.sb("PS%d" % i, [128, 512], BF16) for i in range(4)]
        self.PQ = [self.sb("PQ%d" % i, [128, 512], BF16) for i in range(4)]
        self.maskE = self.sb("maskE", [128, 19 * 128], BF16)
        self.rl = self.sb("rl", [128, 1024], F32)
        self.cq = self.rl[:].rearrange("p (c t) -> p c t", c=2)
        self.csQ = self.sb("csQ", [128, TOK], F32)
        self.eps = self.sb("eps", [128, 1], F32)
        S = self.S
        io = self.io
        S.add("dve", lambda e: e.memset(self.eps[:], EPS), writes=["eps"])
        self.dma("sp", self.maskE[:], io["maskE"], reads=[], writes=["maskE"])
        self.OT = self.scr[:, 0:8192].rearrange("p (h t) -> p h t", h=NH)
        self.qx = self.sb("qx", [128, 4096], BF16)
        self.QAs = [self.scr[:, 8192:10240], self.qx[:, 0:2048]]
        self.QRs = [self.scr[:, 10240:12288], self.qx[:, 2048:4096]]
        for ci in range(4 * nch):
            r, n = ci % 4, ci // 4
            self.dma("act", self.KTl[:, ci * 512:(ci + 1) * 512], io["kvl_g"][r, :, n * TOK:(n + 1) * TOK],
                     reads=[], writes=[("KTl", ci)])
            self.dma("act", self.KRd[:, ci * 512:(ci + 1) * 512], io["kvr_g"][r, :, n * TOK:(n + 1) * TOK],
                     reads=[], writes=[("KRd", ci)])
            self.dma("act", self.V[:, ci * 512:(ci + 1) * 512], io["kvv_g"][r, n], reads=[], writes=[("V", ci)])

    def evac(self, out, in_, reads, writes):
        self.cp_cnt += 1
        if self.cp_cnt % 2:
            self.S.add("act", lambda e: e.activation(out, in_, AF.Copy), reads=reads, writes=writes)
        else:
            self.S.add("dve", lambda e: e.tensor_copy(out, in_), reads=reads, writes=writes)

    def attention(self, j, n):
        S = self.S
        hT, uT, banks = self.hT, self.uT, self.banks
        base = PLAN[("attn", j)]
        N = TOK
        self.norm_h("mix_norm", (2 + j) * 8, N)
        buf, skey = self.fetch(base, KC * 256)
        v = buf[:, 0:KC * 256].rearrange("p (k c) -> p k c", k=KC)
        cqb = []
        for cc in range(2):
            b = self.next_bank()
            cqb.append(b)
            for kc in range(KC):
                self.mm(b, banks[b][:, 0:N], v[:, kc, cc * 128:(cc + 1) * 128], uT[:, kc, :], kc == 0, kc == KC - 1,
                        reads=[skey, ("u", kc)])
            S.add("act", lambda e, b=b, cc=cc: e.activation(self.cq[:, cc, :], banks[b][:, 0:N], AF.Copy),
                  reads=[("ps", b)], writes=[("rl", cc)])
        self.norm([self.cq[:, cc, :] for cc in range(2)], [("rl", cc) for cc in range(2)],
                  [self.cgain("q_lora_norm", j * 2 + cc) for cc in range(2)],
                  [self.cqn[:, cc, :] for cc in range(2)], [("cqn", cc) for cc in range(2)], N, 256.0)
        OT = self.OT
        J = 16 * (n + 1)
        def proj_steps(qb, pbanks):
            par = qb % 2
            QAx, QRx = self.QAs[par], self.QRs[par]
            QA3x = QAx.rearrange("p (h q) -> p h q", h=NH)
            QR3x = QRx.rearrange("p (h q) -> p h q", h=NH)
            ka = (lambda g: ("scr", 16 + g)) if par == 0 else (lambda g: ("qx", g))
            kr = (lambda g: ("scr", 20 + g)) if par == 0 else (lambda g: ("qx", 4 + g))
            qc = slice(qb * 128, (qb + 1) * 128)
            b0, k0 = self.fetch(base + 1, NH * 128)
            b1, k1 = self.fetch(base + 2, NH * 128)
            w0 = b0[:, 0:NH * 128].rearrange("p (h d) -> p h d", h=NH)
            w1 = b1[:, 0:NH * 128].rearrange("p (h d) -> p h d", h=NH)
            for g in range(4):
                b = self.next_bank(pbanks)
                for hh in range(4):
                    h = 4 * g + hh
                    o = banks[b][:, hh * 128:(hh + 1) * 128]
                    self.mm(b, o, w0[:, h, :], self.cqn[:, 0, qc], True, False, reads=[k0, ("cqn", 0)])
                    self.mm(b, o, w1[:, h, :], self.cqn[:, 1, qc], False, True, reads=[k1, ("cqn", 1)])
                self.evac(QAx[:, g * 512:(g + 1) * 512], banks[b][:, 0:512], [("ps", b)], [ka(g)])
                yield
            bk_, kk = self.fetch(base + 5, NH * 128)
            wk = bk_[:, 0:NH * 128].rearrange("p (h r) -> p h r", h=NH)
            for g in range(4):
                b = self.next_bank(pbanks)
                for hh in range(4):
                    h = 4 * g + hh
                    self.mm(b, banks[b][:, hh * 128:(hh + 1) * 128], wk[:, h, :], QA3x[:, h, :], True, True,
                            reads=[kk, ka(g)])
                self.evac(QAx[:, g * 512:(g + 1) * 512], banks[b][:, 0:512], [("ps", b)], [ka(g)])
                yield
            b0, k0 = self.fetch(base + 3, NH * 128)
            b1, k1 = self.fetch(base + 4, NH * 128)
            w0 = b0[:, 0:NH * 128].rearrange("p (h d) -> p h d", h=NH)
            w1 = b1[:, 0:NH * 128].rearrange("p (h d) -> p h d", h=NH)
            for g in range(4):
                b = self.next_bank(pbanks)
                for hh in range(4):
                    h = 4 * g + hh
                    o = banks[b][:, hh * 128:(hh + 1) * 128]
                    self.mm(b, o, w0[:, h, :], self.cqn[:, 0, qc], True, False, reads=[k0, ("cqn", 0)])
                    self.mm(b, o, w1[:, h, :], self.cqn[:, 1, qc], False, True, reads=[k1, ("cqn", 1)])
                S.add("dve", lambda e, b=b, g=g: e.tensor_tensor(
                    QR3x[:, 4 * g:4 * g + 4, :], banks[b][:, 0:512].rearrange("p (h q) -> p h q", h=4),
                    self.csQ[:, qc].unsqueeze(1).broadcast_to([128, 4, 128]), ALU.mult),
                    reads=[("ps", b), "csQ"], writes=[kr(g)])
                yield

        gen = proj_steps(0, range(8))
        for _ in gen:
            pass
        for qb in range(4):
            qc = slice(qb * 128, (qb + 1) * 128)
            par = qb % 2
            QA, QR = self.QAs[par], self.QRs[par]
            kA = (lambda g: ("scr", 16 + g)) if par == 0 else (lambda g: ("qx", g))
            kR = (lambda g: ("scr", 20 + g)) if par == 0 else (lambda g: ("qx", 4 + g))
            gen = proj_steps(qb + 1, [7]) if qb < 3 else iter(())
            tick_every = max(1, (4 * (16 * n + 13 + qb)) // 14)
            tick_cnt = [0]
            Jq = 16 * n + 13 + qb
            for ps_ in range(2):
                its = [(jt, g) for jt in range(Jq) for g in range(2)]
                nit = len(its)
                NP = len(self.P)

                nq4 = Jq // 4

                def s_part(it):
                    jt, g = its[it]
                    hg = 2 * ps_ + g
                    sbk = 4 + it % 3
                    kcols = slice(jt * 128, (jt + 1) * 128)
                    self.mm(sbk, banks[sbk][:, 0:512], self.KTl[:, kcols], QA[:, hg * 512:(hg + 1) * 512], True, False,
                            reads=[("KTl", jt // 4), kA(hg)])
                    self.mm(sbk, banks[sbk][:, 0:512], self.KRd[:, kcols], QR[:, hg * 512:(hg + 1) * 512], False, True,
                            reads=[("KRd", jt // 4), kR(hg)])
                    P = self.P[it % NP]
                    S.add("act", lambda e: e.activation(P[:], banks[sbk][:, 0:512], AF.Exp, scale=SM_SCALE),
                          reads=[("ps", sbk)], writes=[("P", it % NP)])
                    if jt >= 16 * n + qb:
                        eidx = (jt - 16 * n) - qb + 3
                        mk = self.maskE[:, eidx * 128:(eidx + 1) * 128].unsqueeze(1).broadcast_to([128, 4, 128])
                        P3 = P[:].rearrange("p (h q) -> p h q", h=4)
                        S.add("dve", lambda e: e.tensor_tensor(P3, P3, mk, ALU.mult),
                              reads=[("P", it % NP), "maskE"], writes=[("P", it % NP)])
                    acc = self.rl[:, g * 512:(g + 1) * 512]

                    def acc_add(src, skey, first):
                        def emit_it():
                            if first:
                                S.add("dve", lambda e: e.tensor_copy(acc, src[:]), reads=[skey], writes=[("rl", g)])
                            else:
                                S.add("dve", lambda e: e.tensor_tensor(acc, acc, src[:], ALU.add),
                                      reads=[skey, ("rl", g)], writes=[("rl", g)])
                        pend.append((it + 4, emit_it))
                    if jt % 2 == 1:
                        pa = (it - 2) % NP
                        pi = (jt // 2 * 2 + g) % 4
                        PSb = self.PS[pi]
                        Pa = self.P[pa]
                        S.add("dve", lambda e: e.tensor_tensor(PSb[:], Pa[:], P[:], ALU.add),
                              reads=[("P", pa), ("P", it % NP)], writes=[("PS", pi)])
                        if jt % 4 == 3:
                            pj = ((jt // 2 - 1) * 2 + g) % 4
                            qi = (jt // 4 % 2) * 2 + g
                            PQb = self.PQ[qi]
                            PSa = self.PS[pj]
                            S.add("pool", lambda e: e.tensor_tensor(PQb[:], PSa[:], PSb[:], ALU.add),
                                  reads=[("PS", pj), ("PS", pi)], writes=[("PQ", qi)])
                            acc_add(PQb, ("PQ", qi), jt == 3)
                        elif jt >= 4 * nq4:
                            acc_add(PSb, ("PS", pi), False)
                    elif jt == Jq - 1 and jt >= 4 * nq4:
                        acc_add(P, ("P", it % NP), False)

                ob = 2 * (self.pass_idx % 2)
                self.pass_idx += 1

                def pv_part(it):
                    jt, g = its[it]
                    P = self.P[it % NP]
                    self.mm(ob + g, banks[ob + g][:, 0:512], self.V[:, jt * 128:(jt + 1) * 128], P[:], jt == 0,
                            jt == Jq - 1, reads=[("V", jt // 4), ("P", it % NP)])

                pend = []
                prev_tail = self.deferred
                self.deferred = []
                for it in range(nit + 2):
                    while prev_tail and prev_tail[0][0] <= it:
                        prev_tail.pop(0)[1]()
                    while pend and pend[0][0] <= it:
                        pend.pop(0)[1]()
                    if it < nit:
                        s_part(it)
                        tick_cnt[0] += 1
                        if tick_cnt[0] % tick_every == 0:
                            next(gen, None)
                    if it >= 2:
                        pv_part(it - 2)
                while prev_tail:
                    prev_tail.pop(0)[1]()
                while pend:
                    pend.pop(0)[1]()

                def mk_tail(g, ob=ob, ps_=ps_, qc=qc):
                    acc = self.rl[:, g * 512:(g + 1) * 512]
                    hi, lo = self.PQ[g], self.PQ[2 + g]
                    rc = self.sg[g]

                    def split():
                        S.add("dve", lambda e: e.tensor_copy(hi[:], acc), reads=[("rl", g)], writes=[("PQ", g)])
                        S.add("dve", lambda e: e.tensor_tensor(lo[:], acc, hi[:], ALU.subtract),
                              reads=[("rl", g), ("PQ", g)], writes=[("PQ", 2 + g)])

                    def finish():
                        self.mm(7, banks[7][:, 0:512], self.ones[:], hi[:], True, False, reads=["ones", ("PQ", g)])
                        self.mm(7, banks[7][:, 0:512], self.ones[:], lo[:], False, True, reads=["ones", ("PQ", 2 + g)])
                        S.add("dve", lambda e: e.reciprocal(rc[:], banks[7][:, 0:512]), reads=[("ps", 7)],
                              writes=[("sg", g)])
                        for hh in range(4):
                            h = 8 * ps_ + 4 * g + hh
                            S.add("dve", lambda e, hh=hh, h=h: e.tensor_tensor(
                                OT[:, h, qc], banks[ob + g][:, hh * 128:(hh + 1) * 128],
                                rc[:, hh * 128:(hh + 1) * 128], ALU.mult),
                                reads=[("ps", ob + g), ("sg", g)], writes=[("scr", h)])
                    return split, finish
                s0, f0 = mk_tail(0)
                s1, f1 = mk_tail(1)
                self.deferred = [(2, s0), (3, s1), (4, f0), (6, f1)]
            for _ in gen:
                pass
        while self.deferred:
            self.deferred.pop(0)[1]()
        bv, kv_ = self.fetch(base + 6, NH * 128)
        wv = bv[:, 0:NH * 128].rearrange("p (h d) -> p h d", h=NH)
        for h in range(NH):
            b = self.next_bank()
            self.mm(b, banks[b][:, 0:N], wv[:, h, :], OT[:, h, :], True, True, reads=[kv_, ("scr", h)])
            self.evac(OT[:, h, :], banks[b][:, 0:N], [("ps", b)], [("scr", h)])
        for dc in range(KC):
            bo_, ko = self.fetch(base + 7 + dc, NH * 128)
            wo = bo_[:, 0:NH * 128].rearrange("p (h d) -> p h d", h=NH)
            b = self.next_bank()
            for h in range(NH):
                self.mm(b, banks[b][:, 0:N], wo[:, h, :], OT[:, h, :], h == 0, h == NH - 1, reads=[ko, ("scr", h)])
            S.add("dve", lambda e, b=b, dc=dc: e.tensor_tensor(hT[:, dc, :], banks[b][:, 0:N], hT[:, dc, :], ALU.add),
                  reads=[("ps", b), ("h", dc)], writes=[("h", dc)])

    def phase_b(self, convert_slabs, rawwaits=()):
        S = self.S
        if rawwaits:
            S.add("sp", None, rawwaits=rawwaits)
        nch = self.nch
        io = self.io
        self.setup_common(nring=3, nsg=2)
        hT = self.hT
        self.phase_b_setup()
        hkeys = [("h", kc) for kc in range(KC)]
        if convert_slabs:
            stages = [hT[:].rearrange("p a b -> p (a b)"), self.scr[:, 0:8192].bitcast(F32)]
            skeys = [hkeys, [("scr", i) for i in range(16)]]
            outs = [self.ring[1][:], self.ring[2][:]]
            okeys = [[("slab", 1)], [("slab", 2)]]
            self.convert(convert_slabs, stages, skeys, outs, okeys)
        for n in range(nch):
            for kc in range(KC):
                self.dma("act" if kc % 2 else "sp", hT[:, kc, :], io["hA_in"][n][:, kc, :], reads=[], writes=[("h", kc)],
                         chan=self.tile_ch[kc])
            self.dma("sp", self.csQ[0:64, :], io["ropeK"][0, 0:64, n * TOK:(n + 1) * TOK], reads=[], writes=["csQ"])
            self.dma("sp", self.csQ[64:128, :], io["ropeK"][1, 64:128, n * TOK:(n + 1) * TOK], reads=[], writes=["csQ"])
            for l in (2, 3):
                self.ffn(l, "pre", TOK)
                self.attention(l - 2, n)
                self.ffn(l, "post", TOK)
            outs = [hT[:, kc, :] for kc in range(KC)]
            self.norm_h("final_norm", 0, TOK, out_fp32=(outs, hkeys))
            for kc in range(KC):
                self.dma("act" if kc % 2 else "sp", io["out"][n][:, kc, :], hT[:, kc, :], reads=[("h", kc)],
                         writes=[("out", n, kc)], chan=self.tile_ch[kc])

    def finish(self):
        lasts = [c.last for c in self.S.chans if c.last is not None]
        self.S.add("sp", None, extra=lasts)
        self.S.emit(self.nc, self.stack, self.semstack)


ALL_SLABS = list(range(NSLAB))


def build_program(nch, mode, slabs_in):
    nc = bass.Bass("TRN2", target_bir_lowering=False)
    stack = contextlib.ExitStack()
    B = Builder(nc, stack, nch)
    B.slab_map = {s: i for i, s in enumerate(slabs_in)}

    def dt(name, shape, dtp, kind):
        return nc.dram_tensor(name, list(shape), dtp, kind=kind).ap()

    io = {}
    io["wts"] = dt("wts", [len(slabs_in), 128, SLAB], F32, "ExternalInput")
    io["wbf"] = dt("wbf", [NSLAB, 128, SLAB], BF16, "Internal")
    io["consts"] = dt("consts", [128, NCONST], F32, "ExternalInput")
    io["ropeK"] = dt("ropeK", [2, 128, nch * TOK], F32, "ExternalInput")
    if mode == "A":
        io["xm"] = dt("xm", [nch, 128, KC, TOK], F32, "ExternalInput")
        io["xh"] = dt("xh", [128, KC, nch * HALO], F32, "ExternalInput")
        io["poolw"] = dt("poolw", [128, SLAB], F32, "ExternalInput")
        io["pcorr"] = dt("pcorr", [128, 4, 16], F32, "ExternalInput")
        io["ident"] = dt("ident", [128, 128], F32, "ExternalInput")
        io["hA"] = dt("hA", [nch, 128, KC, TOK], F32, "ExternalOutput")
        io["kvl"] = dt("kvl", [128, nch * TOK], BF16, "ExternalOutput")
        io["kvr"] = dt("kvr", [128, nch * TOK], BF16, "ExternalOutput")
        io["kvv"] = dt("kvv", [nch, 128, TOK], BF16, "ExternalOutput")
        B.io = io
        B.phase_a(slabs_in)
    else:
        io["hA_in"] = dt("hA_in", [nch, 128, KC, TOK], F32, "ExternalInput")
        io["kvl_g"] = dt("kvl_g", [4, 128, nch * TOK], BF16, "ExternalInput")
        io["kvr_g"] = dt("kvr_g", [4, 128, nch * TOK], BF16, "ExternalInput")
        io["kvv_g"] = dt("kvv_g", [4, nch, 128, TOK], BF16, "ExternalInput")
        io["maskE"] = dt("maskE", [128, 19 * 128], BF16, "ExternalInput")
        io["out"] = dt("out", [nch, 128, KC, TOK], F32, "ExternalOutput")
        B.io = io
        B.phase_b(slabs_in)
    B.finish()
    stack.close()
    return nc


def build_fused(nch):
    nc = bass.Bass("TRN2", target_bir_lowering=False)
    semstack = contextlib.ExitStack()

    def dt(name, shape, dtp, kind):
        return nc.dram_tensor(name, list(shape), dtp, kind=kind).ap()

    io = {}
    io["wts"] = dt("wts", [NSLAB, 128, SLAB], F32, "ExternalInput")
    io["wbf"] = dt("wbf", [NSLAB, 128, SLAB], BF16, "Internal")
    io["consts"] = dt("consts", [128, NCONST], F32, "ExternalInput")
    io["ropeK"] = dt("ropeK", [2, 128, nch * TOK], F32, "ExternalInput")
    io["xm"] = dt("xm", [nch, 128, KC, TOK], F32, "ExternalInput")
    io["xh"] = dt("xh", [128, KC, nch * HALO], F32, "ExternalInput")
    io["poolw"] = dt("poolw", [128, SLAB], F32, "ExternalInput")
    io["pcorr"] = dt("pcorr", [128, 4, 16], F32, "ExternalInput")
    io["ident"] = dt("ident", [128, 128], F32, "ExternalInput")
    io["maskE"] = dt("maskE", [128, 19 * 128], BF16, "ExternalInput")
    io["hA"] = dt("hA", [nch, 128, KC, TOK], F32, "Internal")
    kvl_t = nc.dram_tensor("kvl", [128, nch * TOK], BF16)
    kvr_t = nc.dram_tensor("kvr", [128, nch * TOK], BF16)
    kvv_t = nc.dram_tensor("kvv", [nch * 128, TOK], BF16)
    kvl_gt = nc.dram_tensor("kvl_g", [4 * 128, nch * TOK], BF16)
    kvr_gt = nc.dram_tensor("kvr_g", [4 * 128, nch * TOK], BF16)
    kvv_gt = nc.dram_tensor("kvv_g", [4 * nch * 128, TOK], BF16)
    io["kvl"] = kvl_t.ap()
    io["kvr"] = kvr_t.ap()
    io["kvv"] = kvv_t.ap().rearrange("(n p) t -> n p t", n=nch)
    kvl_g, kvr_g, kvv_g = kvl_gt.ap(), kvr_gt.ap(), kvv_gt.ap()
    io["out"] = dt("out", [nch, 128, KC, TOK], F32, "ExternalOutput")

    stackA = contextlib.ExitStack()
    A = Builder(nc, stackA, nch, tag="a_", semstack=semstack)
    A.slab_map = {s: s for s in ALL_SLABS}
    A.io = io
    A.phase_a(ALL_SLABS)
    A.finish()
    stackA.close()

    groups = [[0, 1, 2, 3], [4, 5, 6, 7]]
    csems = [semstack.enter_context(nc.semaphore("cc%d" % i)) for i in range(3)]
    with nc.Block() as cblk:
        @cblk.gpsimd
        def _(g):
            for i, (src, dst) in enumerate(((kvl_t, kvl_gt), (kvr_t, kvr_gt), (kvv_t, kvv_gt))):
                g.collective_compute("AllGather", mybir.AluOpType.bypass, replica_groups=groups,
                                     ins=[src.ap().opt()], outs=[dst.ap().opt()]).then_inc(csems[i])
            for i in range(3):
                g.wait_ge(csems[i], 1)

    ioB = dict(io)
    ioB["hA_in"] = io["hA"]
    ioB["kvl_g"] = kvl_g.rearrange("(r p) t -> r p t", r=4)
    ioB["kvr_g"] = kvr_g.rearrange("(r p) t -> r p t", r=4)
    ioB["kvv_g"] = kvv_g.rearrange("(r n p) t -> r n p t", r=4, n=nch)
    stackB = contextlib.ExitStack()
    Bb = Builder(nc, stackB, nch, tag="b_", semstack=semstack)
    Bb.slab_map = {s: s for s in ALL_SLABS}
    Bb.io = ioB
    Bb.phase_b([])
    Bb.finish()
    stackB.close()
    semstack.close()
    return nc


def to_fm(xt):
    T = xt.shape[0]
    return np.ascontiguousarray(xt.T.reshape(KC, 128, T).transpose(1, 0, 2))


def from_fm(a):
    T = a.shape[2]
    return a.transpose(2, 1, 0).reshape(T, D)


def core_inputs(x, nch, W, consts, poolw, cc, ss):
    kk = np.arange(128)[:, None]
    qq = np.arange(128)[None, :]
    ident = np.eye(128, dtype=np.float32)
    ins = []
    for core in range(8):
        b, c = core // 4, core % 4
        xm = np.zeros((nch, 128, KC, TOK), np.float32)
        xh = np.zeros((128, KC, nch * HALO), np.float32)
        pos = np.zeros(nch * TOK, np.int64)
        for n in range(nch):
            ci = 4 * n + c
            xm[n] = to_fm(x[b, ci * TOK:(ci + 1) * TOK])
            if ci > 0:
                xh[:, :, n * HALO:(n + 1) * HALO] = to_fm(x[b, ci * TOK - HALO:ci * TOK])
            pos[n * TOK:(n + 1) * TOK] = np.arange(ci * TOK, (ci + 1) * TOK)
        ropeK = np.ascontiguousarray(np.stack([cc[:, pos], ss[:, pos]], 0))
        pcorr = np.ones((128, 4, 16), np.float32)
        if c == 0:
            for g, w in enumerate(POOL_W):
                pcorr[:, g, :] = (w / np.minimum(np.arange(16) + 1.0, w)).astype(np.float32)[None, :]
        maskE = np.zeros((128, 19, 128), np.float32)
        for e in range(-3, 16):
            maskE[:, e + 3, :] = (((e - 4 * c) * 128 + kk) <= qq)
        maskE = maskE.reshape(128, 19 * 128).astype(ml_dtypes.bfloat16)
        ins.append({"wts": W, "consts": consts, "ropeK": ropeK, "xm": xm, "xh": xh, "poolw": poolw,
                    "pcorr": pcorr, "ident": ident, "maskE": maskE})
    return ins


def kernel(**inp):
    x = np.asarray(inp["x"], np.float32)
    Bn, S, _ = x.shape
    nch = S // (4 * TOK)
    assert Bn == 2 and S == 4 * nch * TOK
    W = host_slabs(inp)
    consts = host_consts(inp)
    poolw = host_poolw(inp)
    cc, ss = rope_tables_host(S)
    ins = core_inputs(x, nch, W, consts, poolw, cc, ss)
    nc = build_fused(nch)
    res = run_bass_kernel_spmd(nc, ins, core_ids=list(range(8))).results
    out = np.zeros((Bn, S, D), np.float32)
    for core in range(8):
        b, c = core // 4, core % 4
        o = np.asarray(res[core]["out"])
        for n in range(nch):
            ci = 4 * n + c
            out[b, ci * TOK:(ci + 1) * TOK] = from_fm(o[n])
    return out
```
